# Optimizing a Trainium2 kernel written in Bass

```python
import math
import jax
import jax.numpy as jnp
from jax import lax
import numpy as np

D_MODEL = 2048
BATCH = 2
SEQ = 16384
DEPTH = 2

HEAD_DIM = 64
NSA_HEADS = 16
NSA_KV_HEADS = 2
NSA_HPG = NSA_HEADS // NSA_KV_HEADS
CMP_LEN = 32
CMP_STRIDE = 16
CMP_HIDDEN = 128
SLC_BLOCK = 64
N_SELECT = 16
NSA_WINDOW = 512
FORCE_BONUS = 1e4
SWA_HEADS = 16
SWA_KV_HEADS = 2
SWA_HPG = SWA_HEADS // SWA_KV_HEADS
SWA_WINDOW = 128
REL_BUCKETS = 32
REL_EXACT = 16
REL_MAX_DIST = 2048
PEER_HEADS = 8
PEER_NKEYS = 128
PEER_EXPERTS = PEER_NKEYS * PEER_NKEYS
PEER_QDIM = 256
PEER_TOPK = 16
PEER_CHUNK = 128

Q_BLOCK = 128
NEG = -1e30
RMS_EPS = 1e-6
NSA_Q_W = NSA_HEADS * HEAD_DIM
NSA_KV_W = NSA_KV_HEADS * HEAD_DIM
NSA_GATE_W = NSA_HEADS * 3
SWA_Q_W = SWA_HEADS * HEAD_DIM
SWA_KV_W = SWA_KV_HEADS * HEAD_DIM
MIX_WIDTH = NSA_Q_W + SWA_Q_W
PROJ_SPLITS = (NSA_Q_W, NSA_KV_W, NSA_KV_W, NSA_KV_W, NSA_KV_W, NSA_KV_W, NSA_KV_W, NSA_GATE_W, SWA_Q_W, SWA_KV_W, SWA_KV_W)
PROJ_WIDTH = sum(PROJ_SPLITS)
SLC_OFFSETS = (-1, 0, 1, 2, 3)
SLC_WEIGHTS = (1.0, 2.0, 2.0, 2.0, 1.0)

kernel_name = 'nsa_swa_sink_peer_hybrid'


def rmsnorm(x, g):
    xf = x.astype(jnp.float32)
    y = xf * lax.rsqrt(jnp.mean(xf * xf, axis=-1, keepdims=True) + RMS_EPS)
    return (y * g.astype(jnp.float32)).astype(x.dtype)


def rel_bucket(dist):
    d = jnp.maximum(dist, 0)
    n_log = REL_BUCKETS - REL_EXACT
    large = REL_EXACT + (jnp.log(jnp.maximum(d, 1).astype(jnp.float32) / REL_EXACT)
                         / math.log(REL_MAX_DIST / REL_EXACT) * n_log).astype(jnp.int32)
    large = jnp.minimum(large, REL_BUCKETS - 1)
    return jnp.where(d < REL_EXACT, d, large)


def masked_softmax(logits, valid):
    p = jax.nn.softmax(jnp.where(valid, logits, NEG), axis=-1)
    return jnp.where(valid, p, 0.0)


def compress(k, pos, w1, w2):
    B, S, G, Dh = k.shape
    chunks = k.reshape(B, S // CMP_STRIDE, CMP_STRIDE, G, Dh)
    blocks = jnp.concatenate([chunks[:, :-1], chunks[:, 1:]], axis=2)
    blocks = blocks + pos[None, None, :, None, :]
    flat = jnp.moveaxis(blocks, 3, 2).reshape(B, blocks.shape[1], G, CMP_LEN * Dh)
    return jax.nn.gelu(flat @ w1) @ w2


def hybrid_mixer(xn, w_in, cmp_pos_k, cmp_w1_k, cmp_w2_k, cmp_pos_v, cmp_w1_v, cmp_w2_v,
                 sinks, rel_bias, w_out):
    B, S, _ = xn.shape
    f32 = jnp.float32
    GA, GB = NSA_KV_HEADS, SWA_KV_HEADS
    NS = S // SLC_BLOCK
    n_sel = min(N_SELECT, NS)
    scale = HEAD_DIM ** -0.5
    offs = [int(c) for c in np.cumsum(PROJ_SPLITS)[:-1]]
    (q_a, kc, vc, ks, vs, kw, vw, gate_a, q_b, k_b, v_b) = jnp.split(xn @ w_in, offs, axis=-1)

    q_a = q_a.reshape(B, S, GA, NSA_HPG, HEAD_DIM)
    q_b = q_b.reshape(B, S, GB, SWA_HPG, HEAD_DIM)
    gate_a = jax.nn.sigmoid(gate_a).reshape(B, S, GA, NSA_HPG, 3)
    kc_c = compress(kc.reshape(B, S, GA, HEAD_DIM), cmp_pos_k, cmp_w1_k, cmp_w2_k)
    vc_c = compress(vc.reshape(B, S, GA, HEAD_DIM), cmp_pos_v, cmp_w1_v, cmp_w2_v)
    n_cmp = kc_c.shape[1]
    cmp_end = jnp.arange(n_cmp) * CMP_STRIDE + CMP_LEN - 1
    ks_blk = ks.reshape(B, NS, SLC_BLOCK, GA, HEAD_DIM).transpose(0, 3, 1, 2, 4)
    vs_blk = vs.reshape(B, NS, SLC_BLOCK, GA, HEAD_DIM).transpose(0, 3, 1, 2, 4)
    pad_a = ((0, 0), (NSA_WINDOW, 0), (0, 0), (0, 0))
    kw_pad = jnp.pad(kw.reshape(B, S, GA, HEAD_DIM), pad_a)
    vw_pad = jnp.pad(vw.reshape(B, S, GA, HEAD_DIM), pad_a)
    pad_b = ((0, 0), (SWA_WINDOW, 0), (0, 0), (0, 0))
    kb_pad = jnp.pad(k_b.reshape(B, S, GB, HEAD_DIM), pad_b)
    vb_pad = jnp.pad(v_b.reshape(B, S, GB, HEAD_DIM), pad_b)

    bias_a = rel_bias[:, :NSA_HEADS].reshape(REL_BUCKETS, GA, NSA_HPG).astype(f32)
    bias_b = rel_bias[:, NSA_HEADS:].reshape(REL_BUCKETS, GB, SWA_HPG).astype(f32)
    sink = sinks.astype(f32).reshape(1, GB, SWA_HPG, 1, 1)
    b_idx = jnp.arange(B)[:, None, None, None]
    g_idx = jnp.arange(GA)[None, :, None, None]
    blk_ids = jnp.arange(NS)
    slc_tok = jnp.arange(SLC_BLOCK)

    def token_bias(table, dist):
        return table[rel_bucket(dist)].transpose(2, 3, 0, 1)

    def query_block(i):
        q0 = i * Q_BLOCK
        pos_q = q0 + jnp.arange(Q_BLOCK)
        qa = lax.dynamic_slice_in_dim(q_a, q0, Q_BLOCK, axis=1)
        dt = qa.dtype

        dist_c = pos_q[:, None] - cmp_end[None, :]
        lg_c = jnp.einsum('bqghd,bcgd->bghqc', qa, kc_c, preferred_element_type=f32) * scale
        p_cmp = masked_softmax(lg_c + token_bias(bias_a, dist_c), dist_c >= 0)
        o_cmp = jnp.einsum('bghqc,bcgd->bqghd', p_cmp.astype(dt), vc_c)

        imp = jnp.pad(p_cmp.sum(axis=2), ((0, 0), (0, 0), (0, 0), (1, 1)))
        imp_s = jnp.zeros(imp.shape[:-1] + (NS,), f32)
        for r, w in zip(SLC_OFFSETS, SLC_WEIGHTS):
            imp_s = imp_s + w * imp[..., r + 1: r + 2 + 4 * (NS - 1): 4]
        cb = pos_q[:, None] // SLC_BLOCK
        j = blk_ids[None, :]
        forced = (j == 0) | (j == cb) | (j == cb - 1)
        score = jnp.where(j > cb, NEG, imp_s + jnp.where(forced, FORCE_BONUS, 0.0))
        _, sel = lax.top_k(score, n_sel)
        ks_g = ks_blk[b_idx, g_idx, sel].reshape(B, GA, Q_BLOCK, n_sel * SLC_BLOCK, HEAD_DIM)
        vs_g = vs_blk[b_idx, g_idx, sel].reshape(B, GA, Q_BLOCK, n_sel * SLC_BLOCK, HEAD_DIM)
        kpos = (sel[..., None] * SLC_BLOCK + slc_tok).reshape(B, GA, Q_BLOCK, n_sel * SLC_BLOCK)
        dist_s = pos_q[:, None] - kpos
        bias_s = jnp.moveaxis(bias_a[rel_bucket(dist_s), g_idx], -1, 2)
        lg_s = jnp.einsum('bqghd,bgqkd->bghqk', qa, ks_g, preferred_element_type=f32) * scale
        p_s = masked_softmax(lg_s + bias_s, (dist_s >= 0)[:, :, None])
        o_slc = jnp.einsum('bghqk,bgqkd->bqghd', p_s.astype(dt), vs_g)

        La = Q_BLOCK + NSA_WINDOW
        kwin = lax.dynamic_slice_in_dim(kw_pad, q0, La, axis=1)
        vwin = lax.dynamic_slice_in_dim(vw_pad, q0, La, axis=1)
        kpos_w = q0 - NSA_WINDOW + jnp.arange(La)
        dist_w = pos_q[:, None] - kpos_w[None, :]
        valid_w = (dist_w >= 0) & (dist_w < NSA_WINDOW) & (kpos_w[None, :] >= 0)
        lg_w = jnp.einsum('bqghd,blgd->bghql', qa, kwin, preferred_element_type=f32) * scale
        p_w = masked_softmax(lg_w + token_bias(bias_a, dist_w), valid_w)
        o_win = jnp.einsum('bghql,blgd->bqghd', p_w.astype(dt), vwin)

        g = lax.dynamic_slice_in_dim(gate_a, q0, Q_BLOCK, axis=1)
        o_a = g[..., 0:1] * o_cmp + g[..., 1:2] * o_slc + g[..., 2:3] * o_win

        qb = lax.dynamic_slice_in_dim(q_b, q0, Q_BLOCK, axis=1)
        Lb = Q_BLOCK + SWA_WINDOW
        kbw = lax.dynamic_slice_in_dim(kb_pad, q0, Lb, axis=1)
        vbw = lax.dynamic_slice_in_dim(vb_pad, q0, Lb, axis=1)
        kpos_b = q0 - SWA_WINDOW + jnp.arange(Lb)
        dist_b = pos_q[:, None] - kpos_b[None, :]
        valid_b = (dist_b >= 0) & (dist_b < SWA_WINDOW) & (kpos_b[None, :] >= 0)
        lg_b = jnp.einsum('bqghd,blgd->bghql', qb, kbw, preferred_element_type=f32) * scale
        lg_b = jnp.where(valid_b, lg_b + token_bias(bias_b, dist_b), NEG)
        m = jnp.maximum(lg_b.max(axis=-1, keepdims=True), sink)
        e = jnp.exp(lg_b - m)
        p_b = e / (e.sum(axis=-1, keepdims=True) + jnp.exp(sink - m))
        o_b = jnp.einsum('bghql,blgd->bqghd', p_b.astype(dt), vbw)

        return jnp.concatenate([o_a.reshape(B, Q_BLOCK, NSA_Q_W),
                                o_b.reshape(B, Q_BLOCK, SWA_Q_W)], axis=-1)

    out = lax.map(query_block, jnp.arange(S // Q_BLOCK))
    out = out.transpose(1, 0, 2, 3).reshape(B, S, MIX_WIDTH)
    return out @ w_out


def peer(xn, wq, subkeys, u, v):
    B, S, D = xn.shape
    T = B * S
    xt = xn.reshape(T, D)
    q = (xt @ wq).reshape(T, PEER_HEADS, 2, PEER_QDIM // 2).astype(jnp.float32)
    s1 = jnp.einsum('thd,nd->thn', q[:, :, 0], subkeys[0].astype(jnp.float32))
    s2 = jnp.einsum('thd,nd->thn', q[:, :, 1], subkeys[1].astype(jnp.float32))
    v1, i1 = lax.top_k(s1, PEER_TOPK)
    v2, i2 = lax.top_k(s2, PEER_TOPK)
    cand = (v1[..., :, None] + v2[..., None, :]).reshape(T, PEER_HEADS, PEER_TOPK * PEER_TOPK)
    sc, ci = lax.top_k(cand, PEER_TOPK)
    e1 = jnp.take_along_axis(i1, ci // PEER_TOPK, axis=-1)
    e2 = jnp.take_along_axis(i2, ci % PEER_TOPK, axis=-1)
    idx = e1 * PEER_NKEYS + e2
    gate = jax.nn.softmax(sc, axis=-1).astype(xn.dtype)
    n_ch = T // PEER_CHUNK

    def chunk(args):
        xc, ic, gc = args
        h = jax.nn.gelu(jnp.einsum('cd,chkd->chk', xc, u[ic]))
        return jnp.einsum('chk,chkd->cd', gc * h, v[ic])

    out = lax.map(chunk, (xt.reshape(n_ch, PEER_CHUNK, D),
                          idx.reshape(n_ch, PEER_CHUNK, PEER_HEADS, PEER_TOPK),
                          gate.reshape(n_ch, PEER_CHUNK, PEER_HEADS, PEER_TOPK)))
    return out.reshape(B, S, D)


def setup_inputs(seed: int = 0) -> dict:
    key = jax.random.key(seed)
    ks = jax.random.split(key, 18)

    def nrm(k, shape, s):
        return jax.random.normal(k, shape, jnp.float32) * s

    return {
        'x': nrm(ks[0], (BATCH, SEQ, D_MODEL), 1.0),
        'attn_norm': 1.0 + nrm(ks[1], (DEPTH, D_MODEL), 0.02),
        'w_in': nrm(ks[2], (DEPTH, D_MODEL, PROJ_WIDTH), D_MODEL ** -0.5),
        'cmp_pos_k': nrm(ks[3], (DEPTH, CMP_LEN, HEAD_DIM), 0.1),
        'cmp_w1_k': nrm(ks[4], (DEPTH, CMP_LEN * HEAD_DIM, CMP_HIDDEN), (CMP_LEN * HEAD_DIM) ** -0.5),
        'cmp_w2_k': nrm(ks[5], (DEPTH, CMP_HIDDEN, HEAD_DIM), CMP_HIDDEN ** -0.5),
        'cmp_pos_v': nrm(ks[6], (DEPTH, CMP_LEN, HEAD_DIM), 0.1),
        'cmp_w1_v': nrm(ks[7], (DEPTH, CMP_LEN * HEAD_DIM, CMP_HIDDEN), (CMP_LEN * HEAD_DIM) ** -0.5),
        'cmp_w2_v': nrm(ks[8], (DEPTH, CMP_HIDDEN, HEAD_DIM), CMP_HIDDEN ** -0.5),
        'sinks': nrm(ks[9], (DEPTH, SWA_HEADS), 0.5),
        'w_out': nrm(ks[10], (DEPTH, MIX_WIDTH, D_MODEL), MIX_WIDTH ** -0.5),
        'ffn_norm': 1.0 + nrm(ks[11], (DEPTH, D_MODEL), 0.02),
        'peer_wq': nrm(ks[12], (DEPTH, D_MODEL, PEER_HEADS * PEER_QDIM), D_MODEL ** -0.5),
        'peer_subkeys': nrm(ks[13], (DEPTH, 2, PEER_NKEYS, PEER_QDIM // 2), (PEER_QDIM // 2) ** -0.5),
        'peer_u': nrm(ks[14], (DEPTH, PEER_EXPERTS, D_MODEL), D_MODEL ** -0.5),
        'peer_v': nrm(ks[15], (DEPTH, PEER_EXPERTS, D_MODEL), (PEER_HEADS * PEER_TOPK) ** -0.5),
        'rel_bias': nrm(ks[16], (REL_BUCKETS, NSA_HEADS + SWA_HEADS), 0.5),
        'final_norm': 1.0 + nrm(ks[17], (D_MODEL,), 0.02),
    }


def reference(x, attn_norm, w_in, cmp_pos_k, cmp_w1_k, cmp_w2_k, cmp_pos_v, cmp_w1_v, cmp_w2_v,
              sinks, w_out, ffn_norm, peer_wq, peer_subkeys, peer_u, peer_v, rel_bias, final_norm):
    h = x
    for l in range(DEPTH):
        h = h + hybrid_mixer(rmsnorm(h, attn_norm[l]), w_in[l],
                             cmp_pos_k[l], cmp_w1_k[l], cmp_w2_k[l],
                             cmp_pos_v[l], cmp_w1_v[l], cmp_w2_v[l],
                             sinks[l], rel_bias, w_out[l])
        h = h + peer(rmsnorm(h, ffn_norm[l]), peer_wq[l], peer_subkeys[l], peer_u[l], peer_v[l])
    return rmsnorm(h, final_norm)
```

```python
import math
import types
import numpy as np
import concourse.bass as bass
import concourse.mybir as mybir
from concourse.bass_utils import run_bass_kernel_spmd
from contextlib import ExitStack

F32 = mybir.dt.float32
BF16 = mybir.dt.bfloat16
I32 = mybir.dt.int32
U32 = mybir.dt.uint32
AF = mybir.ActivationFunctionType
ALU = mybir.AluOpType
AX = mybir.AxisListType

D = 2048
KC = 16
DEPTH = 2
N_CORES = 8
RMS_EPS = 1e-6
NTAB = 103
NREL = 13
BIGNEG = -240000.0
NFM = 15
NTM = 198
F_KC, F_VC, F_KS, F_KW, F_KB = 10, 11, 12, 13, 14


class Cfg:
    def __init__(self, S=16384, NKEYS=128, B=2):
        self.S, self.NKEYS, self.B = S, NKEYS, B
        self.T = B * S
        self.NT = S // 128
        self.NC = S // 16 - 1
        self.NCP = ((self.NC + 127) // 128) * 128
        self.NS = S // 64
        self.NJG = (self.NS + 127) // 128
        self.NCH = max(1, self.T // 4096)
        self.CR = self.T // self.NCH
        self.PR = self.CR // 8
        self.TOKL = self.T // 8
        self.NEXP = NKEYS * NKEYS
        self.ESH = self.NEXP // 8


class Seq:
    ENGS = ("tensor", "vector", "scalar", "gpsimd", "sync")

    def __init__(self):
        self.ops = []

    def add(self, eng, fn, kind="c"):
        if fn.__closure__:
            cells = tuple(types.CellType(c.cell_contents) for c in fn.__closure__)
            fn = types.FunctionType(fn.__code__, fn.__globals__, fn.__name__, fn.__defaults__, cells)
        self.ops.append((eng, fn, kind))

    def pe(self, fn): self.add("tensor", fn)
    def dve(self, fn): self.add("vector", fn)
    def act(self, fn): self.add("scalar", fn)
    def pool(self, fn): self.add("gpsimd", fn)
    def dma(self, fn): self.add("sync", fn, "d")
    def pdma(self, fn): self.add("gpsimd", fn, "d")
    def cc(self, fn): self.add("gpsimd", fn, "c")

    def emit(self, sems, block):
        plan = []
        cnt = {e: 0 for e in self.ENGS}
        prev = None
        for (eng, fn, kind) in self.ops:
            w = (prev, cnt[prev]) if prev is not None else None
            inc = 16 if kind == "d" else 1
            cnt[eng] += inc
            plan.append((eng, fn, w, inc))
            prev = eng
        final = (prev, cnt[prev]) if prev is not None else None

        def make(engname):
            def body(e):
                lastwait = {}
                for (eng, fn, w, inc) in plan:
                    if eng != engname:
                        continue
                    if w is not None and lastwait.get(w[0], -1) < w[1]:
                        e.wait_ge(sems[w[0]], w[1])
                        lastwait[w[0]] = w[1]
                    fn(e).then_inc(sems[eng], inc)
                if final is not None:
                    e.wait_ge(sems[final[0]], final[1])
            return body

        block.tensor(make("tensor"))
        block.vector(make("vector"))
        block.scalar(make("scalar"))
        block.gpsimd(make("gpsimd"))
        block.sync(make("sync"))


def build_nc(cfg, stop_after=None):
    S, T, NT, NC, NCP, NS, NJG = cfg.S, cfg.T, cfg.NT, cfg.NC, cfg.NCP, cfg.NS, cfg.NJG
    NCH, CR, PR, TOKL, NK, NEXP, ESH = cfg.NCH, cfg.CR, cfg.PR, cfg.TOKL, cfg.NKEYS, cfg.NEXP, cfg.ESH
    nc = bass.Bass("TRN2", target_bir_lowering=False)

    def din(name, shape, dt=F32):
        return nc.dram_tensor(name, list(shape), dt, kind="ExternalInput").ap()

    def dint(name, shape, dt=F32):
        return nc.dram_tensor(name, list(shape), dt)

    x_in = din("x", [TOKL, D])
    an_in = din("an", [DEPTH, D]); fn_in = din("fn", [DEPTH, D]); fin_in = din("fin", [1, D])
    wsel_in = din("wsel", [DEPTH, D, NFM * 64 + NTM])
    wo_in = din("wo", [DEPTH, 256, D])
    wq_in = din("wq_full", [DEPTH, D, D])
    pu_in = [din(f"pu_full{l_}", [NEXP, D]) for l_ in range(DEPTH)]
    pv_in = [din(f"pv_full{l_}", [NEXP, D]) for l_ in range(DEPTH)]
    skt_in = din("skT", [DEPTH, 2, 128, NK])
    w1_in = {"k": din("w1k", [DEPTH, 2048, 128]), "v": din("w1v", [DEPTH, 2048, 128])}
    w2_in = {"k": din("w2k", [DEPTH, 128, 64]), "v": din("w2v", [DEPTH, 128, 64])}
    pos_in = {"k": din("posk", [DEPTH, 2048]), "v": din("posv", [DEPTH, 2048])}
    sink_in = din("sink2", [DEPTH, 2])
    b31c_in = din("b31c", [1, 8]); b31s_in = din("b31s", [1, 2])
    tcr_in = din("tcr", [8, 128, NTAB]); mc_in = din("mc", [128, NTAB])
    tsel_in = din("tselr", [NREL, 128, 2, 128]); tswa_in = din("tswar", [2, 128, 2, 128])
    m0_in = din("m0", [128, 128]); m1_in = din("m1", [128, 128])
    gexp_in = din("gexp", [128, 8192]); ident_in = din("ident", [128, 128])
    iota_in = din("iota16", [128, 16]); bon_in = din("bon", [128, 3])
    y_out = nc.dram_tensor("y", [TOKL, D], F32, kind="ExternalOutput").ap()

    ikind = {"kind": "ExternalOutput"} if stop_after in ("proj", "attn") else {}
    BIN = dint("ag_in", [128, D]); STAGE = dint("ag_out", [N_CORES * 128, D])
    red = [dint(f"red{k}", [PR, D]) for k in range(NCH)]
    part = [nc.dram_tensor(f"part{k}", [CR, D], F32, **ikind) for k in range(NCH)]
    h_loc = [dint(f"h_loc{k}", [PR, D]) for k in range(NCH)]
    h_full = [dint(f"h_full{k}", [CR, D]) for k in range(NCH)]
    FMt = nc.dram_tensor("FM", [64, NFM, T], BF16, **ikind); FM = FMt.ap()
    TMt = nc.dram_tensor("TM", [T, 192], BF16, **ikind); TM = TMt.ap()
    GTt_ = nc.dram_tensor("GT", [T, 6], F32, **ikind); GT = GTt_.ap()
    XN = dint("XN", [128, D]).ap()
    PUb = dint("PUb", [NEXP, D], BF16); PVb = dint("PVb", [NEXP, D], BF16)

    ALLR = [list(range(N_CORES))]
    s = Seq()

    def rows(lst, per, g0, n=128):
        k, off = divmod(g0, per)
        return lst[k].ap()[off:off + n, :]

    R0 = 128

    def gather_rows(src_fn, SH, dst):
        for j in range(SH // R0):
            s.dma(lambda e, j=j: e.dma_start(out=BIN.ap()[:, :], in_=src_fn(j)))
            s.cc(lambda e: e.collective_compute("AllGather", ALU.bypass, replica_groups=ALLR,
                                                ins=[BIN.ap().opt()], outs=[STAGE.ap().opt()]))
            s.dma(lambda e, j=j: e.dma_start(
                out=dst.ap().rearrange("(r s) d -> r s d", r=N_CORES)[:, j * R0:(j + 1) * R0, :],
                in_=STAGE.ap().rearrange("(r s) d -> r s d", r=N_CORES)))

    with ExitStack() as top:
        uid = [0]

        def SB(es, name, shape, dt=F32):
            uid[0] += 1
            return es.enter_context(nc.sbuf_tensor(f"{name}_{uid[0]}", list(shape), dt))

        def PS(es, name, shape, dt=F32):
            uid[0] += 1
            return es.enter_context(nc.psum_tensor(f"{name}_{uid[0]}", list(shape), dt))

        sems = {e: top.enter_context(nc.semaphore("s_" + e)) for e in Seq.ENGS}
        identf = SB(top, "identf", [128, 128]); identb = SB(top, "identb", [128, 128], BF16)
        st = SB(top, "st", [128, 16])
        s.dma(lambda e: e.dma_start(out=identf[:], in_=ident_in[:, :]))
        s.dve(lambda e: e.tensor_copy(out=identb[:], in_=identf[:]))
        for k in range(NCH):
            s.dma(lambda e, k=k: e.dma_start(out=h_loc[k].ap()[:, :], in_=x_in[k * PR:(k + 1) * PR, :]))

        def rmsnorm(src, g, dst, junk):
            s.act(lambda e: e.activation(out=junk[:], in_=src[:], func=AF.Square, accum_out=st[:, 0:1]))
            s.dve(lambda e: e.tensor_scalar(out=st[:, 1:2], in0=st[:, 0:1], scalar1=1.0 / D, scalar2=RMS_EPS,
                                            op0=ALU.mult, op1=ALU.add))
            s.act(lambda e: e.activation(out=st[:, 2:3], in_=st[:, 1:2], func=AF.Sqrt))
            s.dve(lambda e: e.reciprocal(out=st[:, 3:4], in_=st[:, 2:3]))
            s.dve(lambda e: e.scalar_tensor_tensor(out=dst[:], in0=src[:], scalar=st[:, 3:4], in1=g[:],
                                                   op0=ALU.mult, op1=ALU.mult))

        def gelu(src_ap, bias_ap, out_ap, xg, t1, np_, nf):
            if bias_ap is not None:
                s.act(lambda e: e.activation(out=xg[0:np_, 0:nf], in_=src_ap, func=AF.Identity, bias=bias_ap))
            else:
                s.act(lambda e: e.copy(out=xg[0:np_, 0:nf], in_=src_ap))
            s.dve(lambda e: e.tensor_tensor(out=t1[0:np_, 0:nf], in0=xg[0:np_, 0:nf], in1=xg[0:np_, 0:nf], op=ALU.mult))
            s.dve(lambda e: e.tensor_scalar(out=t1[0:np_, 0:nf], in0=t1[0:np_, 0:nf], scalar1=0.044715, scalar2=1.0,
                                            op0=ALU.mult, op1=ALU.add))
            s.dve(lambda e: e.tensor_tensor(out=t1[0:np_, 0:nf], in0=t1[0:np_, 0:nf], in1=xg[0:np_, 0:nf], op=ALU.mult))
            s.act(lambda e: e.activation(out=t1[0:np_, 0:nf], in_=t1[0:np_, 0:nf], func=AF.Tanh,
                                         scale=0.7978845608028654))
            s.dve(lambda e: e.scalar_tensor_tensor(out=t1[0:np_, 0:nf], in0=t1[0:np_, 0:nf], scalar=1.0,
                                                   in1=xg[0:np_, 0:nf], op0=ALU.add, op1=ALU.mult))
            s.act(lambda e: e.mul(out_ap, t1[0:np_, 0:nf], 0.5))

        def finish_early():
            with ExitStack() as es:
                bA = SB(es, "fe_bA", [128, D])
                for r0 in range(0, TOKL, 128):
                    kk, off = divmod(r0, PR)
                    s.dma(lambda e, kk=kk, off=off: e.dma_start(out=bA[:], in_=h_loc[kk].ap()[off:off + 128, :]))
                    s.dma(lambda e, r0=r0: e.dma_start(out=y_out[r0:r0 + 128, :], in_=bA[:]))

        stopped = False
        for l in range(DEPTH):
            if stopped:
                break
            last = (l == DEPTH - 1)
            for k in range(NCH):
                s.cc(lambda e, k=k: e.collective_compute("AllGather", ALU.bypass, replica_groups=ALLR,
                                                         ins=[h_loc[k].ap().opt()], outs=[h_full[k].ap().opt()]))

            with ExitStack() as es:
                WFb = SB(es, "WFb", [128, KC, NFM * 64], BF16); WTb = SB(es, "WTb", [128, KC, NTM], BF16)
                wst = SB(es, "wst", [128, NFM * 64 + NTM])
                G = SB(es, "G", [128, D]); xt = SB(es, "xt", [128, D]); sq = SB(es, "sq", [128, D])
                ub = SB(es, "ub", [128, D], BF16); uT = SB(es, "uT", [128, KC, 512], BF16)
                stf = SB(es, "stf", [64, 512], BF16); stv = SB(es, "stv", [128, 192], BF16); stg = SB(es, "stg", [128, 6])
                pt = PS(es, "pt", [128, 128], BF16); pf = PS(es, "pf", [128, 512]); pv = PS(es, "pv", [128, 512])
                for kc in range(KC):
                    s.dma(lambda e, kc=kc: e.dma_start(out=wst[:], in_=wsel_in[l, kc * 128:(kc + 1) * 128, :]))
                    s.dve(lambda e, kc=kc: e.tensor_copy(out=WFb[:, kc, :], in_=wst[:, 0:NFM * 64]))
                    s.dve(lambda e, kc=kc: e.tensor_copy(out=WTb[:, kc, :], in_=wst[:, NFM * 64:NFM * 64 + NTM]))
                s.dma(lambda e: e.dma_start(out=G[:], in_=an_in[l:l + 1, :].partition_broadcast(128)))
                for tg in range(T // 512):
                    for tt in range(4):
                        g0 = tg * 512 + tt * 128
                        s.dma(lambda e, g0=g0: e.dma_start(out=xt[:], in_=rows(h_full, CR, g0)))
                        rmsnorm(xt, G, ub, sq)
                        for kc in range(KC):
                            s.pe(lambda e, kc=kc: e.transpose(pt[:], ub[:, kc * 128:(kc + 1) * 128], identb[:]))
                            s.act(lambda e, kc=kc, tt=tt: e.copy(out=uT[:, kc, tt * 128:(tt + 1) * 128], in_=pt[:]))
                    for f in range(NFM):
                        for kc in range(KC):
                            s.pe(lambda e, kc=kc, f=f: e.matmul(pf[0:64, :], lhsT=WFb[:, kc, f * 64:(f + 1) * 64],
                                                               rhs=uT[:, kc, :], start=(kc == 0), stop=(kc == KC - 1)))
                        s.act(lambda e: e.copy(out=stf[:], in_=pf[0:64, :]))
                        s.dma(lambda e, f=f, tg=tg: e.dma_start(out=FM[:, f, tg * 512:(tg + 1) * 512], in_=stf[:]))
                    for tt in range(4):
                        g0 = tg * 512 + tt * 128
                        for kc in range(KC):
                            s.pe(lambda e, kc=kc, tt=tt: e.matmul(pv[:, 0:NTM], lhsT=uT[:, kc, tt * 128:(tt + 1) * 128],
                                                                 rhs=WTb[:, kc, :], start=(kc == 0), stop=(kc == KC - 1)))
                        s.act(lambda e: e.copy(out=stv[:], in_=pv[:, 0:192]))
                        s.act(lambda e: e.activation(out=stg[:], in_=pv[:, 192:198], func=AF.Sigmoid))
                        s.dma(lambda e, g0=g0: e.dma_start(out=TM[g0:g0 + 128, :], in_=stv[:]))
                        s.dma(lambda e, g0=g0: e.dma_start(out=GT[g0:g0 + 128, :], in_=stg[:]))
            if stop_after == "proj":
                finish_early(); stopped = True
                break

            with ExitStack() as lay:
                KCC = [SB(lay, f"KCC{b}", [64, NCP], BF16) for b in range(cfg.B)]
                VCC = [SB(lay, f"VCC{b}", [128, NCP // 128, 64], BF16) for b in range(cfg.B)]
                with ExitStack() as es:
                    w1s = SB(es, "w1s", [64, 32, 128]); W1b = SB(es, "W1b", [64, 32, 128], BF16)
                    W1c = SB(es, "W1c", [128, 16, 128]); posc = SB(es, "posc", [128, 16, 2])
                    w2s = SB(es, "w2s", [128, 64]); W2b = SB(es, "W2b", [128, 64], BF16)
                    PB = SB(es, "PB", [128, 2]); BIG = SB(es, "BIG", [64, S], BF16)
                    HID = SB(es, "HID", [128, NCP], BF16)
                    xg = SB(es, "xg", [128, 512]); t1 = SB(es, "t1", [128, 512])
                    ph = PS(es, "ph", [128, 512]); p2 = PS(es, "p2", [128, 512]); pb = PS(es, "pbb", [128, 512])
                    for which in ("k", "v"):
                        s.dma(lambda e, which=which: e.dma_start(
                            out=w1s[:], in_=w1_in[which][l].rearrange("(p d) h -> d p h", d=64)))
                        s.dve(lambda e: e.tensor_copy(out=W1b[:], in_=w1s[:]))
                        s.dma(lambda e, which=which: e.dma_start(
                            out=W1c[:], in_=w1_in[which][l].rearrange("(c q) h -> q c h", q=128)))
                        s.dve(lambda e: e.memset(posc[:], 0.0))
                        s.dma(lambda e, which=which: e.dma_start(
                            out=xg[:, 0:16], in_=pos_in[which][l].rearrange("(c q) -> q c", q=128),
                            allow_slow_non_contiguous=True))
                        s.dve(lambda e: e.tensor_copy(out=posc[:, :, 0], in_=xg[:, 0:16]))
                        for c in range(16):
                            s.pe(lambda e, c=c: e.matmul(pb[:, 0:2], lhsT=W1c[:, c, :], rhs=posc[:, c, :],
                                                         start=(c == 0), stop=(c == 15)))
                        s.act(lambda e: e.copy(out=PB[:], in_=pb[:, 0:2]))
                        s.dma(lambda e, which=which: e.dma_start(out=w2s[:], in_=w2_in[which][l]))
                        s.dve(lambda e: e.tensor_copy(out=W2b[:], in_=w2s[:]))
                        fidx = F_KC if which == "k" else F_VC
                        for b in range(cfg.B):
                            s.dma(lambda e, b=b, fidx=fidx: e.dma_start(out=BIG[:], in_=FM[:, fidx, b * S:(b + 1) * S]))
                            for n0 in range(0, NC, 512):
                                nn = min(512, NC - n0)
                                for p in range(32):
                                    s.pe(lambda e, p=p, n0=n0, nn=nn: e.matmul(
                                        ph[:, 0:nn], lhsT=W1b[:, p, :],
                                        rhs=BIG[:, 16 * n0 + p:16 * (n0 + nn - 1) + p + 1:16],
                                        start=(p == 0), stop=(p == 31)))
                                gelu(ph[:, 0:nn], PB[:, 0:1], HID[:, n0:n0 + nn], xg, t1, 128, nn)
                            if which == "k":
                                for n0 in range(0, NC, 512):
                                    nn = min(512, NC - n0)
                                    s.pe(lambda e, n0=n0, nn=nn: e.matmul(p2[0:64, 0:nn], lhsT=W2b[:], rhs=HID[:, n0:n0 + nn],
                                                                         start=True, stop=True))
                                    s.act(lambda e, n0=n0, nn=nn, b=b: e.copy(out=KCC[b][:, n0:n0 + nn], in_=p2[0:64, 0:nn]))
                            else:
                                for cc in range((NC + 127) // 128):
                                    w = min(128, NC - cc * 128)
                                    s.pe(lambda e, cc=cc, w=w: e.matmul(p2[0:w, 0:64], lhsT=HID[:, cc * 128:cc * 128 + w],
                                                                       rhs=W2b[:], start=True, stop=True))
                                    s.act(lambda e, cc=cc, w=w, b=b: e.copy(out=VCC[b][0:w, cc, :], in_=p2[0:w, 0:64]))

                with ExitStack() as es:
                    B31C = SB(es, "B31C", [128, 8]); B31S = SB(es, "B31S", [128, 2])
                    TcM = SB(es, "TcM", [128, 8, NTAB]); MC = SB(es, "MC", [128, NTAB])
                    stgT = SB(es, "stgT", [128, NREL, 2, 128])
                    TSEL = SB(es, "TSEL", [128, NREL, 2, 128], BF16); TWIN = SB(es, "TWIN", [128, 5, 2, 128], BF16)
                    TSWA = SB(es, "TSWA", [128, 2, 2, 128], BF16)
                    M0 = SB(es, "M0", [128, 128]); M1 = SB(es, "M1", [128, 128])
                    SINKE = SB(es, "SINKE", [128, 2]); BON = SB(es, "BON", [128, 3])
                    GEXP = SB(es, "GEXP", [128, 8192], BF16); gst = SB(es, "gst", [128, 2048])
                    WO = SB(es, "WO", [128, 2, D], BF16)
                    KS = SB(es, "KS", [64, S], BF16); VS = SB(es, "VS", [128, NT, 65], BF16)
                    QA = SB(es, "QA", [64, 8, 128], BF16); QB = SB(es, "QB", [64, 2, 128], BF16)
                    GTt = SB(es, "GTt", [128, 6])
                    KWt = SB(es, "KWt", [64, 640], BF16); VWt = SB(es, "VWt", [128, 5, 65], BF16)
                    KBt = SB(es, "KBt", [64, 256], BF16); VBt = SB(es, "VBt", [128, 2, 65], BF16)
                    E = SB(es, "E", [128, 1024]); Z = SB(es, "Z", [128, NTAB])
                    IMP = SB(es, "IMP", [128, 1040]); SC = SB(es, "SC", [128, 256]); SC2 = SB(es, "SC2", [128, 256])
                    M8 = SB(es, "M8", [128, 16]); SM = SB(es, "SM", [128, 8]); RS8 = SB(es, "RS8", [128, 8])
                    NEGSEL = SB(es, "NEGSEL", [128, NJG * 128]); NST = SB(es, "NST", [128, NJG, 2, 128], BF16)
                    ET = SB(es, "ET", [128, 128], BF16); PT = SB(es, "PT", [128, 256], BF16)
                    OCS = SB(es, "OCS", [128, 2, 64]); OSS = SB(es, "OSS", [128, 256]); OWS = SB(es, "OWS", [128, 256])
                    OBS = SB(es, "OBS", [128, 256]); OT = SB(es, "OT", [128, 256]); OTT = SB(es, "OTT", [128, 2, 128], BF16)
                    CF = SB(es, "CF", [128, 8]); STG = SB(es, "STG", [128, 512])
                    CL = PS(es, "CL", [128, 1024]); SL = PS(es, "SL", [128, 512]); OSp = PS(es, "OSp", [128, 512])
                    OWp = PS(es, "OWp", [128, 512]); OBp = PS(es, "OBp", [128, 512]); TP = PS(es, "TP", [128, 512])
                    OC = PS(es, "OC", [128, 512])

                    s.dma(lambda e: e.dma_start(out=B31C[:], in_=b31c_in[0:1, :].partition_broadcast(128)))
                    s.dma(lambda e: e.dma_start(out=B31S[:], in_=b31s_in[0:1, :].partition_broadcast(128)))
                    s.dma(lambda e: e.dma_start(out=TcM[:], in_=tcr_in.rearrange("h q x -> q h x")))
                    s.dma(lambda e: e.dma_start(out=MC[:], in_=mc_in[:, :]))
                    s.dma(lambda e: e.dma_start(out=M0[:], in_=m0_in[:, :]))
                    s.dma(lambda e: e.dma_start(out=M1[:], in_=m1_in[:, :]))
                    s.dma(lambda e: e.dma_start(out=BON[:], in_=bon_in[:, :]))
                    for h in range(8):
                        s.dve(lambda e, h=h: e.scalar_tensor_tensor(out=TcM[:, h, :], in0=TcM[:, h, :], scalar=B31C[:, h:h + 1],
                                                                    in1=MC[:], op0=ALU.subtract, op1=ALU.add))
                    s.dma(lambda e: e.dma_start(out=stgT[:], in_=tsel_in.rearrange("r k h q -> k r h q")))
                    for h in range(2):
                        s.dve(lambda e, h=h: e.tensor_scalar(out=stgT[:, :, h, :], in0=stgT[:, :, h, :], scalar1=B31C[:, h:h + 1],
                                                             scalar2=8.0, op0=ALU.subtract, op1=ALU.mult))
                        s.dve(lambda e, h=h: e.tensor_tensor(out=stgT[:, 0, h, :], in0=stgT[:, 0, h, :], in1=M0[:], op=ALU.add))
                    s.dve(lambda e: e.tensor_copy(out=TSEL[:], in_=stgT[:]))
                    for h in range(2):
                        s.dve(lambda e, h=h: e.tensor_tensor(out=stgT[:, 4, h, :], in0=stgT[:, 4, h, :], in1=M1[:], op=ALU.add))
                    s.dve(lambda e: e.tensor_copy(out=TWIN[:], in_=stgT[:, 0:5, :, :]))
                    s.dma(lambda e: e.dma_start(out=stgT[:, 0:2, :, :], in_=tswa_in.rearrange("r k h q -> k r h q")))
                    for h in range(2):
                        s.dve(lambda e, h=h: e.tensor_scalar(out=stgT[:, 0:2, h, :], in0=stgT[:, 0:2, h, :], scalar1=B31S[:, h:h + 1],
                                                             scalar2=8.0, op0=ALU.subtract, op1=ALU.mult))
                        s.dve(lambda e, h=h: e.tensor_tensor(out=stgT[:, 0, h, :], in0=stgT[:, 0, h, :], in1=M0[:], op=ALU.add))
                        s.dve(lambda e, h=h: e.tensor_tensor(out=stgT[:, 1, h, :], in0=stgT[:, 1, h, :], in1=M1[:], op=ALU.add))
                    s.dve(lambda e: e.tensor_copy(out=TSWA[:], in_=stgT[:, 0:2, :, :]))
                    s.dma(lambda e: e.dma_start(out=SINKE[:], in_=sink_in[l:l + 1, :].partition_broadcast(128)))
                    s.dve(lambda e: e.tensor_tensor(out=SINKE[:], in0=SINKE[:], in1=B31S[:], op=ALU.subtract))
                    s.act(lambda e: e.activation(out=SINKE[:], in_=SINKE[:], func=AF.Exp))
                    for q4 in range(4):
                        s.dma(lambda e, q4=q4: e.dma_start(out=gst[:], in_=gexp_in[:, q4 * 2048:(q4 + 1) * 2048]))
                        s.dve(lambda e, q4=q4: e.tensor_copy(out=GEXP[:, q4 * 2048:(q4 + 1) * 2048], in_=gst[:]))
                    for c in range(2):
                        s.dma(lambda e, c=c: e.dma_start(out=gst[:], in_=wo_in[l, c * 128:(c + 1) * 128, :]))
                        s.dve(lambda e, c=c: e.tensor_copy(out=WO[:, c, :], in_=gst[:]))
                    s.dve(lambda e: e.memset(VS[:], 1.0))
                    s.dve(lambda e: e.memset(VWt[:], 1.0))
                    s.dve(lambda e: e.memset(VBt[:], 1.0))

                    def band_attn(i, lo, Kt, Vt, Qt, TAB, Op, OSB):
                        nch = i - lo + 1
                        for idx in range(nch):
                            rel = i - (lo + idx)
                            s.pe(lambda e, idx=idx: e.matmul(SL[:, 0:256], lhsT=Kt[:, idx * 128:(idx + 1) * 128], rhs=Qt,
                                                             start=True, stop=False))
                            s.pe(lambda e, rel=rel: e.matmul(SL[:, 0:256], lhsT=identb[:], rhs=TAB[:, rel, :, :],
                                                             start=False, stop=True))
                            s.act(lambda e: e.activation(out=PT[:], in_=SL[:, 0:256], func=AF.Exp, scale=0.125))
                            for h in range(2):
                                s.pe(lambda e, idx=idx, h=h: e.matmul(
                                    Op[:, h * 128:h * 128 + 65], lhsT=PT[:, h * 128:(h + 1) * 128], rhs=Vt[:, idx, :],
                                    start=(idx == 0 and h == 0), stop=(idx == nch - 1 and h == 1), skip_group_check=True))
                        s.act(lambda e: e.copy(out=OSB[:], in_=Op[:, 0:256]))

                    for b in range(cfg.B):
                        s.dma(lambda e, b=b: e.dma_start(out=KS[:], in_=FM[:, F_KS, b * S:(b + 1) * S]))
                        for i0 in range(0, NT, 16):
                            i1 = min(NT, i0 + 16)
                            s.dma(lambda e, b=b, i0=i0, i1=i1: e.dma_start(
                                out=VS[:, i0:i1, 0:64],
                                in_=TM[b * S + i0 * 128:b * S + i1 * 128, 0:64].rearrange("(i p) d -> p i d", p=128)))
                        for i in range(NT):
                            t0 = b * S + i * 128
                            lo = max(0, i - 4); lob = max(0, i - 1)
                            s.dma(lambda e, t0=t0: e.dma_start(out=QA[:], in_=FM[:, 0:8, t0:t0 + 128]))
                            s.dma(lambda e, t0=t0: e.dma_start(out=QB[:], in_=FM[:, 8:10, t0:t0 + 128]))
                            s.dma(lambda e, t0=t0: e.dma_start(out=GTt[:], in_=GT[t0:t0 + 128, :]))
                            nw = i - lo + 1
                            s.dma(lambda e, b=b, lo=lo, nw=nw: e.dma_start(
                                out=KWt[:, 0:nw * 128], in_=FM[:, F_KW, b * S + lo * 128:b * S + (lo + nw) * 128]))
                            s.dma(lambda e, b=b, lo=lo, nw=nw: e.dma_start(
                                out=VWt[:, 0:nw, 0:64],
                                in_=TM[b * S + lo * 128:b * S + (lo + nw) * 128, 64:128].rearrange("(c p) d -> p c d", p=128)))
                            nb = i - lob + 1
                            s.dma(lambda e, b=b, lob=lob, nb=nb: e.dma_start(
                                out=KBt[:, 0:nb * 128], in_=FM[:, F_KB, b * S + lob * 128:b * S + (lob + nb) * 128]))
                            s.dma(lambda e, b=b, lob=lob, nb=nb: e.dma_start(
                                out=VBt[:, 0:nb, 0:64],
                                in_=TM[b * S + lob * 128:b * S + (lob + nb) * 128, 128:192].rearrange("(c p) d -> p c d", p=128)))

                            n = 8 * i + 7
                            c_lo = max(0, 8 * i - 96)
                            x_lo = c_lo - (8 * i - 96)
                            nnear = n - c_lo
                            s.dve(lambda e: e.memset(IMP[:], 0.0))
                            for h in range(8):
                                for n0 in range(0, n, 512):
                                    nn = min(512, n - n0)
                                    s.pe(lambda e, h=h, n0=n0, nn=nn, b=b: e.matmul(
                                        CL[:, n0:n0 + nn], lhsT=QA[:, h, :], rhs=KCC[b][:, n0:n0 + nn], start=True, stop=True))
                                for (a0, a1) in ([(c_lo, n)] if (c_lo >= 512 or n <= 512) else [(c_lo, 512), (512, n)]):
                                    s.dve(lambda e, h=h, c_lo=c_lo, a0=a0, a1=a1, x_lo=x_lo: e.scalar_tensor_tensor(
                                        out=Z[:, a0 - c_lo:a1 - c_lo], in0=CL[:, a0:a1], scalar=0.125,
                                        in1=TcM[:, h, x_lo + a0 - c_lo:x_lo + a1 - c_lo], op0=ALU.mult, op1=ALU.add))
                                s.act(lambda e, c_lo=c_lo, n=n, nnear=nnear: e.activation(
                                    out=E[:, c_lo:n], in_=Z[:, 0:nnear], func=AF.Exp, accum_out=SM[:, 0:1]))
                                for (a0, a1) in ([(0, c_lo)] if c_lo <= 512 else [(0, 512), (512, c_lo)]):
                                    if a1 > a0:
                                        s.act(lambda e, a0=a0, a1=a1: e.activation(out=E[:, a0:a1], in_=CL[:, a0:a1], func=AF.Exp,
                                                                                   scale=0.125, accum_out=SM[:, 1:2]))
                                        s.dve(lambda e: e.tensor_tensor(out=SM[:, 0:1], in0=SM[:, 0:1], in1=SM[:, 1:2], op=ALU.add))
                                s.dve(lambda e: e.tensor_scalar(out=SM[:, 2:3], in0=SM[:, 0:1], scalar1=1e-30, scalar2=None,
                                                                op0=ALU.max))
                                s.dve(lambda e, h=h: e.reciprocal(out=RS8[:, h:h + 1], in_=SM[:, 2:3]))
                                if h == 0:
                                    s.dve(lambda e, h=h, n=n: e.tensor_scalar(out=IMP[:, 1:1 + n], in0=E[:, 0:n],
                                                                              scalar1=RS8[:, h:h + 1], scalar2=None, op0=ALU.mult))
                                else:
                                    s.dve(lambda e, h=h, n=n: e.scalar_tensor_tensor(
                                        out=IMP[:, 1:1 + n], in0=E[:, 0:n], scalar=RS8[:, h:h + 1], in1=IMP[:, 1:1 + n],
                                        op0=ALU.mult, op1=ALU.add))
                                if h < 2:
                                    ncc = (n + 127) // 128
                                    for cc in range(ncc):
                                        w = min(128, n - cc * 128)
                                        s.pe(lambda e, cc=cc, w=w: e.transpose(TP[0:w, 0:128], E[:, cc * 128:cc * 128 + w], identf[:]))
                                        s.act(lambda e, w=w: e.copy(out=ET[0:w, :], in_=TP[0:w, 0:128]))
                                        s.pe(lambda e, cc=cc, w=w, b=b, ncc=ncc: e.matmul(
                                            OC[:, 0:64], lhsT=ET[0:w, :], rhs=VCC[b][0:w, cc, :],
                                            start=(cc == 0), stop=(cc == ncc - 1)))
                                    s.dve(lambda e, h=h: e.tensor_scalar(out=OCS[:, h, :], in0=OC[:, 0:64], scalar1=RS8[:, h:h + 1],
                                                                         scalar2=None, op0=ALU.mult))

                            W = 2 * i + 2
                            s.dve(lambda e: e.memset(NEGSEL[:], BIGNEG))
                            if W <= 16:
                                s.dve(lambda e, W=W: e.memset(NEGSEL[:, 0:W], 0.0))
                            else:
                                s.dve(lambda e, W=W: e.tensor_copy(out=SC[:, 0:W], in_=IMP[:, 0:4 * W:4]))
                                for r_, wgt in ((1, 2.0), (2, 2.0), (3, 2.0), (4, 1.0)):
                                    s.dve(lambda e, W=W, r_=r_, wgt=wgt: e.scalar_tensor_tensor(
                                        out=SC[:, 0:W], in0=IMP[:, r_:r_ + 4 * W:4], scalar=wgt, in1=SC[:, 0:W],
                                        op0=ALU.mult, op1=ALU.add))
                                s.dve(lambda e: e.tensor_scalar(out=SC[:, 0:1], in0=SC[:, 0:1], scalar1=1e4, scalar2=None, op0=ALU.add))
                                s.dve(lambda e, i=i: e.tensor_tensor(out=SC[:, 2 * i - 1:2 * i + 2], in0=SC[:, 2 * i - 1:2 * i + 2],
                                                                     in1=BON[:], op=ALU.add))
                                s.dve(lambda e, W=W: e.max(out=M8[:, 0:8], in_=SC[:, 0:W]))
                                s.dve(lambda e, W=W: e.match_replace(out=SC2[:, 0:W], in_to_replace=M8[:, 0:8], in_values=SC[:, 0:W],
                                                                     imm_value=-3.0e38))
                                s.dve(lambda e, W=W: e.max(out=M8[:, 8:16], in_=SC2[:, 0:W]))
                                s.dve(lambda e, W=W: e.tensor_scalar(out=SC2[:, 0:W], in0=SC[:, 0:W], scalar1=M8[:, 15:16],
                                                                     scalar2=-BIGNEG, op0=ALU.is_ge, op1=ALU.mult))
                                s.dve(lambda e, W=W: e.tensor_scalar(out=NEGSEL[:, 0:W], in0=SC2[:, 0:W], scalar1=BIGNEG, scalar2=None,
                                                                     op0=ALU.add))
                            for jg in range(NJG):
                                s.pe(lambda e, jg=jg: e.transpose(TP[:, 0:128], NEGSEL[:, jg * 128:(jg + 1) * 128], identf[:]))
                                for h in range(2):
                                    s.act(lambda e, jg=jg, h=h: e.copy(out=NST[:, jg, h, :], in_=TP[:, 0:128]))

                            for ck in range(i + 1):
                                rel = i - ck
                                s.pe(lambda e, ck=ck: e.matmul(SL[:, 0:256], lhsT=KS[:, ck * 128:(ck + 1) * 128], rhs=QA[:, 0:2, :],
                                                               start=True, stop=False))
                                s.pe(lambda e, ck=ck, rel=rel: e.matmul(
                                    SL[:, 0:256], lhsT=GEXP[:, 128 * (ck % 64):128 * (ck % 64) + 128], rhs=NST[:, ck // 64, :, :],
                                    start=False, stop=(rel >= NREL)))
                                if rel < NREL:
                                    s.pe(lambda e, rel=rel: e.matmul(SL[:, 0:256], lhsT=identb[:], rhs=TSEL[:, rel, :, :],
                                                                     start=False, stop=True))
                                s.act(lambda e: e.activation(out=PT[:], in_=SL[:, 0:256], func=AF.Exp, scale=0.125))
                                for h in range(2):
                                    s.pe(lambda e, ck=ck, h=h, i=i: e.matmul(
                                        OSp[:, h * 128:h * 128 + 65], lhsT=PT[:, h * 128:(h + 1) * 128], rhs=VS[:, ck, :],
                                        start=(ck == 0 and h == 0), stop=(ck == i and h == 1), skip_group_check=True))
                            s.act(lambda e: e.copy(out=OSS[:], in_=OSp[:, 0:256]))

                            band_attn(i, lo, KWt, VWt, QA[:, 0:2, :], TWIN, OWp, OWS)
                            band_attn(i, lob, KBt, VBt, QB[:, 0:2, :], TSWA, OBp, OBS)

                            for h in range(2):
                                s.dve(lambda e, h=h: e.reciprocal(out=CF[:, 0:1], in_=OSS[:, h * 128 + 64:h * 128 + 65]))
                                s.dve(lambda e, h=h: e.reciprocal(out=CF[:, 1:2], in_=OWS[:, h * 128 + 64:h * 128 + 65]))
                                s.dve(lambda e, h=h: e.tensor_tensor(out=CF[:, 0:1], in0=CF[:, 0:1], in1=GTt[:, 3 * h + 1:3 * h + 2], op=ALU.mult))
                                s.dve(lambda e, h=h: e.tensor_tensor(out=CF[:, 1:2], in0=CF[:, 1:2], in1=GTt[:, 3 * h + 2:3 * h + 3], op=ALU.mult))
                                s.dve(lambda e, h=h: e.tensor_scalar(out=OT[:, h * 64:(h + 1) * 64], in0=OCS[:, h, :],
                                                                     scalar1=GTt[:, 3 * h:3 * h + 1], scalar2=None, op0=ALU.mult))
                                s.dve(lambda e, h=h: e.scalar_tensor_tensor(
                                    out=OT[:, h * 64:(h + 1) * 64], in0=OSS[:, h * 128:h * 128 + 64], scalar=CF[:, 0:1],
                                    in1=OT[:, h * 64:(h + 1) * 64], op0=ALU.mult, op1=ALU.add))
                                s.dve(lambda e, h=h: e.scalar_tensor_tensor(
                                    out=OT[:, h * 64:(h + 1) * 64], in0=OWS[:, h * 128:h * 128 + 64], scalar=CF[:, 1:2],
                                    in1=OT[:, h * 64:(h + 1) * 64], op0=ALU.mult, op1=ALU.add))
                                s.dve(lambda e, h=h: e.tensor_tensor(out=CF[:, 2:3], in0=OBS[:, h * 128 + 64:h * 128 + 65],
                                                                     in1=SINKE[:, h:h + 1], op=ALU.add))
                                s.dve(lambda e: e.reciprocal(out=CF[:, 3:4], in_=CF[:, 2:3]))
                                s.dve(lambda e, h=h: e.tensor_scalar(out=OT[:, 128 + h * 64:128 + (h + 1) * 64],
                                                                     in0=OBS[:, h * 128:h * 128 + 64], scalar1=CF[:, 3:4],
                                                                     scalar2=None, op0=ALU.mult))
                            for c in range(2):
                                s.pe(lambda e, c=c: e.transpose(TP[:, 0:128], OT[:, c * 128:(c + 1) * 128], identf[:]))
                                s.act(lambda e, c=c: e.copy(out=OTT[:, c, :], in_=TP[:, 0:128]))
                            for cc in range(4):
                                for c in range(2):
                                    s.pe(lambda e, c=c, cc=cc: e.matmul(SL[:, 0:512], lhsT=OTT[:, c, :], rhs=WO[:, c, cc * 512:(cc + 1) * 512],
                                                                       start=(c == 0), stop=(c == 1)))
                                s.act(lambda e: e.copy(out=STG[:], in_=SL[:, 0:512]))
                                s.dma(lambda e, t0=t0, cc=cc: e.dma_start(out=rows(part, CR, t0)[:, cc * 512:(cc + 1) * 512], in_=STG[:]))
            if stop_after == "attn":
                finish_early(); stopped = True
                break

            for k in range(NCH):
                s.cc(lambda e, k=k: e.collective_compute("ReduceScatter", ALU.add, replica_groups=ALLR,
                                                         ins=[part[k].ap().opt()], outs=[red[k].ap().opt()]))

            if stop_after == "rs":
                with ExitStack() as es:
                    bA = SB(es, "rs_bA", [128, D])
                    for r0 in range(0, TOKL, 128):
                        kk, off = divmod(r0, PR)
                        s.dma(lambda e, kk=kk, off=off: e.dma_start(out=bA[:], in_=red[kk].ap()[off:off + 128, :]))
                        s.dma(lambda e, r0=r0: e.dma_start(out=y_out[r0:r0 + 128, :], in_=bA[:]))
                stopped = True
                break
            with ExitStack() as es:
                WQb = SB(es, "WQb", [128, KC, D], BF16)
                bA = SB(es, "bA", [128, D]); bB = SB(es, "bB", [128, D]); bC = SB(es, "bC", [128, D])
                bD = SB(es, "bD", [128, D]); bE = SB(es, "bE", [128, D])
                G2 = SB(es, "G2", [128, D]); GF = SB(es, "GF", [128, D])
                xnb = SB(es, "xnb", [128, D], BF16); xT = SB(es, "xT", [128, KC, 128], BF16)
                QT = SB(es, "QT", [128, KC, 128]); SKT = SB(es, "SKT", [128, 2, NK])
                S12 = SB(es, "S12", [128, NK]); S12b = SB(es, "S12b", [128, NK])
                V12 = SB(es, "V12", [128, 2, 16]); I12 = SB(es, "I12", [128, 2, 16]); IU = SB(es, "IU", [128, 16], U32)
                IU2 = SB(es, "IU2", [128, 16], U32)
                CAND = SB(es, "CAND", [128, 256]); CAND2 = SB(es, "CAND2", [128, 256]); SCV = SB(es, "SCV", [128, 16])
                AFt = SB(es, "AFt", [128, 16]); BFt = SB(es, "BFt", [128, 16]); OH = SB(es, "OH", [128, 16, 16])
                E1 = SB(es, "E1", [128, 16]); E2 = SB(es, "E2", [128, 16]); IOT = SB(es, "IOT", [128, 16])
                IDX = SB(es, "IDX", [128, 128]); GATE = SB(es, "GATE", [128, 128]); GE = SB(es, "GE", [128, 16])
                IDXT = SB(es, "IDXT", [128, 128], I32); GATET = SB(es, "GATET", [128, 128])
                HP = SB(es, "HP", [128, 128]); GH = SB(es, "GH", [128, 128]); xg = SB(es, "xg2", [128, 128]); t1 = SB(es, "t12", [128, 128])
                BB = SB(es, "BB", [128, 256], BF16)
                UGb = SB(es, "UGb", [128, D], BF16); VGb = SB(es, "VGb", [128, D], BF16)
                PO = PS(es, "PO", [128, 2048]); PQ = PS(es, "PQ", [128, 512]); TPf = PS(es, "TPf", [128, 512])
                PSs = PS(es, "PSs", [128, 512]); ptb = PS(es, "ptb", [128, 128], BF16)
                for kc in range(KC):
                    s.dma(lambda e, kc=kc: e.dma_start(out=bE[:], in_=wq_in[l, kc * 128:(kc + 1) * 128, :]))
                    s.dve(lambda e, kc=kc: e.tensor_copy(out=WQb[:, kc, :], in_=bE[:]))
                s.dma(lambda e: e.dma_start(out=SKT[:], in_=skt_in[l].rearrange("a d n -> d a n")))
                s.dma(lambda e: e.dma_start(out=G2[:], in_=fn_in[l:l + 1, :].partition_broadcast(128)))
                s.dma(lambda e: e.dma_start(out=GF[:], in_=fin_in[0:1, :].partition_broadcast(128)))
                s.dma(lambda e: e.dma_start(out=IOT[:], in_=iota_in[:, :]))
                s.dve(lambda e: e.memset(BB[:], 0.0))
                for (src_t, dst_t) in ((pu_in[l], PUb), (pv_in[l], PVb)):
                    for r0_ in range(0, NEXP, 128):
                        s.dma(lambda e, src_t=src_t, r0_=r0_: e.dma_start(out=bE[:], in_=src_t[r0_:r0_ + 128, :]))
                        s.dve(lambda e: e.tensor_copy(out=UGb[:], in_=bE[:]))
                        s.dma(lambda e, dst_t=dst_t, r0_=r0_: e.dma_start(out=dst_t.ap()[r0_:r0_ + 128, :], in_=UGb[:]))

                def top16(src, src2, n_, vout, iout_u):
                    s.dve(lambda e: e.max(out=vout[:, 0:8], in_=src[:, 0:n_]))
                    s.dve(lambda e: e.max_index(out=iout_u[:, 0:8], in_max=vout[:, 0:8], in_values=src[:, 0:n_]))
                    s.dve(lambda e: e.match_replace(out=src2[:, 0:n_], in_to_replace=vout[:, 0:8], in_values=src[:, 0:n_],
                                                    imm_value=-3.0e38))
                    s.dve(lambda e: e.max(out=vout[:, 8:16], in_=src2[:, 0:n_]))
                    s.dve(lambda e: e.max_index(out=iout_u[:, 8:16], in_max=vout[:, 8:16], in_values=src2[:, 0:n_]))

                for lt in range(TOKL // 128):
                    r0 = lt * 128
                    kk, off = divmod(r0, PR)
                    s.dma(lambda e, kk=kk, off=off: e.dma_start(out=bA[:], in_=h_loc[kk].ap()[off:off + 128, :]))
                    s.dma(lambda e, kk=kk, off=off: e.dma_start(out=bB[:], in_=red[kk].ap()[off:off + 128, :]))
                    s.dve(lambda e: e.tensor_tensor(out=bA[:], in0=bA[:], in1=bB[:], op=ALU.add))
                    rmsnorm(bA, G2, bC, bE)
                    s.dma(lambda e: e.dma_start(out=XN[:, :], in_=bC[:]))
                    s.dve(lambda e: e.tensor_copy(out=xnb[:], in_=bC[:]))
                    for kc in range(KC):
                        s.pe(lambda e, kc=kc: e.transpose(ptb[:], xnb[:, kc * 128:(kc + 1) * 128], identb[:]))
                        s.act(lambda e, kc=kc: e.copy(out=xT[:, kc, :], in_=ptb[:]))
                    for cc in range(4):
                        for kc in range(KC):
                            s.pe(lambda e, kc=kc, cc=cc: e.matmul(PQ[:, 0:512], lhsT=xT[:, kc, :], rhs=WQb[:, kc, cc * 512:(cc + 1) * 512],
                                                                 start=(kc == 0), stop=(kc == KC - 1)))
                        s.act(lambda e, cc=cc: e.copy(out=bD[:, cc * 512:(cc + 1) * 512], in_=PQ[:, 0:512]))
                    for c in range(KC):
                        s.pe(lambda e, c=c: e.transpose(TPf[:, 0:128], bD[:, c * 128:(c + 1) * 128], identf[:]))
                        s.act(lambda e, c=c: e.copy(out=QT[:, c, :], in_=TPf[:, 0:128]))
                    for h in range(8):
                        for a in range(2):
                            s.pe(lambda e, h=h, a=a: e.matmul(PSs[:, 0:NK], lhsT=QT[:, 2 * h + a, :], rhs=SKT[:, a, :],
                                                              start=True, stop=True))
                            s.act(lambda e: e.copy(out=S12[:], in_=PSs[:, 0:NK]))
                            top16(S12, S12b, NK, V12[:, a, :], IU)
                            s.dve(lambda e, a=a: e.tensor_copy(out=I12[:, a, :], in_=IU[:]))
                        s.dve(lambda e: e.tensor_tensor(
                            out=CAND[:].rearrange("p (a b) -> p a b", a=16),
                            in0=V12[:, 0, :].unsqueeze(2).to_broadcast([128, 16, 16]),
                            in1=V12[:, 1, :].unsqueeze(1).to_broadcast([128, 16, 16]), op=ALU.add))
                        top16(CAND, CAND2, 256, SCV, IU2)
                        s.dve(lambda e: e.tensor_single_scalar(out=IU[:], in_=IU2[:], scalar=4, op=ALU.logical_shift_right))
                        s.dve(lambda e: e.tensor_copy(out=AFt[:], in_=IU[:]))
                        s.dve(lambda e: e.tensor_single_scalar(out=IU[:], in_=IU2[:], scalar=15, op=ALU.bitwise_and))
                        s.dve(lambda e: e.tensor_copy(out=BFt[:], in_=IU[:]))
                        for (sel_f, a, dst) in ((AFt, 0, E1), (BFt, 1, E2)):
                            s.dve(lambda e, sel_f=sel_f: e.tensor_tensor(
                                out=OH[:], in0=IOT[:].unsqueeze(1).to_broadcast([128, 16, 16]),
                                in1=sel_f[:].unsqueeze(2).to_broadcast([128, 16, 16]), op=ALU.is_equal))
                            s.dve(lambda e, a=a: e.tensor_tensor(
                                out=OH[:], in0=OH[:], in1=I12[:, a, :].unsqueeze(1).to_broadcast([128, 16, 16]), op=ALU.mult))
                            s.dve(lambda e, dst=dst: e.tensor_reduce(out=dst[:], in_=OH[:], axis=AX.X, op=ALU.add))
                        s.dve(lambda e, h=h: e.scalar_tensor_tensor(out=IDX[:, h * 16:(h + 1) * 16], in0=E1[:], scalar=float(NK),
                                                                    in1=E2[:], op0=ALU.mult, op1=ALU.add))
                        s.dve(lambda e: e.tensor_scalar(out=st[:, 8:9], in0=SCV[:, 0:1], scalar1=-1.0, scalar2=None, op0=ALU.mult))
                        s.act(lambda e: e.activation(out=GE[:], in_=SCV[:], func=AF.Exp, bias=st[:, 8:9], accum_out=st[:, 9:10]))
                        s.dve(lambda e: e.reciprocal(out=st[:, 10:11], in_=st[:, 9:10]))
                        s.dve(lambda e, h=h: e.tensor_scalar(out=GATE[:, h * 16:(h + 1) * 16], in0=GE[:], scalar1=st[:, 10:11],
                                                             scalar2=None, op0=ALU.mult))
                    s.pe(lambda e: e.transpose(TPf[:, 0:128], IDX[:], identf[:]))
                    s.act(lambda e: e.copy(out=GH[:], in_=TPf[:, 0:128]))
                    s.dve(lambda e: e.tensor_copy(out=IDXT[:], in_=GH[:]))
                    s.pe(lambda e: e.transpose(TPf[:, 0:128], GATE[:], identf[:]))
                    s.act(lambda e: e.copy(out=GATET[:], in_=TPf[:, 0:128]))
                    if stop_after == "peer_idx":
                        s.dma(lambda e: e.dma_start(out=y_out[0:128, 0:128], in_=IDX[:]))
                        s.dma(lambda e: e.dma_start(out=y_out[0:128, 128:256], in_=GATE[:]))
                        s.dma(lambda e: e.dma_start(out=y_out[0:128, 256:384], in_=GATET[:]))
                        s.dma(lambda e: e.dma_start(out=y_out[128:256, :], in_=bD[:]))
                        s.dma(lambda e: e.dma_start(out=y_out[256:384, :], in_=bB[:]))
                        s.dma(lambda e: e.dma_start(out=y_out[384:512, :], in_=bA[:]))
                        s.dma(lambda e: e.dma_start(out=y_out[0:128, 384:384 + 32], in_=S12[:]))
                        s.dma(lambda e: e.dma_start(out=y_out[0:128, 512:512 + 128], in_=QT[:, 15, :]))
                        break
                    npt = 2 if stop_after == "peer_g2" else 128
                    for t in range(npt):
                        s.pdma(lambda e, t=t: e.indirect_dma_start(
                            out=UGb[:], out_offset=None, in_=PUb.ap()[:, :],
                            in_offset=bass.IndirectOffsetOnAxis(ap=IDXT[:, t:t + 1].bitcast(U32), axis=0)))
                        s.dma(lambda e, t=t: e.dma_start(out=bB[:], in_=XN[t:t + 1, :].partition_broadcast(128)))
                        s.dve(lambda e, t=t: e.scalar_tensor_tensor(out=bE[:], in0=UGb[:], scalar=1.0, in1=bB[:],
                                                                    op0=ALU.mult, op1=ALU.mult, accum_out=HP[:, t:t + 1]))
                    if stop_after == "peer_g2":
                        s.dma(lambda e: e.dma_start(out=y_out[0:128, 0:128], in_=HP[:]))
                        break
                    gelu(HP[:], None, GH[:], xg, t1, 128, 128)
                    s.dve(lambda e: e.tensor_tensor(out=GH[:], in0=GH[:], in1=GATET[:], op=ALU.mult))
                    for t in range(128):
                        s.pdma(lambda e, t=t: e.indirect_dma_start(
                            out=VGb[:], out_offset=None, in_=PVb.ap()[:, :],
                            in_offset=bass.IndirectOffsetOnAxis(ap=IDXT[:, t:t + 1].bitcast(U32), axis=0)))
                        s.act(lambda e, t=t: e.copy(out=BB[:, 127:128], in_=GH[:, t:t + 1]))
                        for cc in range(4):
                            s.pe(lambda e, t=t, cc=cc: e.matmul(PO[:, cc * 512:(cc + 1) * 512], lhsT=BB[:, 127 - t:255 - t],
                                                               rhs=VGb[:, cc * 512:(cc + 1) * 512], start=(t == 0), stop=(t == 127)))
                    s.dve(lambda e: e.tensor_tensor(out=bA[:], in0=bA[:], in1=PO[:], op=ALU.add))
                    if not last and stop_after != "layer0":
                        s.dma(lambda e, kk=kk, off=off: e.dma_start(out=h_loc[kk].ap()[off:off + 128, :], in_=bA[:]))
                    elif stop_after == "layer0":
                        s.dma(lambda e, r0=r0: e.dma_start(out=y_out[r0:r0 + 128, :], in_=bA[:]))
                    else:
                        rmsnorm(bA, GF, bC, bE)
                        s.dma(lambda e, r0=r0: e.dma_start(out=y_out[r0:r0 + 128, :], in_=bC[:]))
            if stop_after in ("layer0", "peer_idx", "peer_g2"):
                stopped = True

        block = top.enter_context(nc.Block())
        s.emit(sems, block)
    return nc, len(s.ops)


def _bucket(d):
    d = np.maximum(d, 0)
    large = 16 + (np.log(np.maximum(d, 1).astype(np.float32) / np.float32(16))
                  / np.float32(math.log(2048 / 16)) * np.float32(16)).astype(np.int32)
    large = np.minimum(large, 31)
    return np.where(d < 16, d, large).astype(np.int64)


def make_inputs(cfg, c, p):
    S, T, NCH, CR, PR, NK, ESH = cfg.S, cfg.T, cfg.NCH, cfg.CR, cfg.PR, cfg.NKEYS, cfg.ESH
    g, hq = c // 4, c % 4
    own = [2 * hq, 2 * hq + 1]
    order8 = own + [h for h in range(8) if h not in own]
    headsA = [g * 8 + h for h in order8]
    headsB = [g * 8 + h for h in own]
    xf = p["x"].reshape(T, D)
    m = {}
    m["x"] = np.concatenate([xf[k * CR + c * PR:k * CR + (c + 1) * PR] for k in range(NCH)], 0)
    m["an"] = p["attn_norm"]; m["fn"] = p["ffn_norm"]; m["fin"] = p["final_norm"].reshape(1, D)
    cols = []
    for h in headsA:
        cols += list(range(h * 64, (h + 1) * 64))
    for h in headsB:
        cols += list(range(1840 + h * 64, 1840 + (h + 1) * 64))
    for base in (1024, 1152, 1280, 1536, 2864):
        cols += list(range(base + g * 64, base + (g + 1) * 64))
    for base in (1408, 1664, 2992):
        cols += list(range(base + g * 64, base + (g + 1) * 64))
    for h in own:
        cols += list(range(1792 + (g * 8 + h) * 3, 1792 + (g * 8 + h) * 3 + 3))
    cols = np.asarray(cols)
    m["wsel"] = p["w_in"][:, :, cols]
    rws = list(range((g * 8 + own[0]) * 64, (g * 8 + own[0]) * 64 + 128))
    rws += [1024 + r for r in rws]
    m["wo"] = p["w_out"][:, rws, :]
    m["wq_full"] = p["peer_wq"];
    for l_ in range(DEPTH):
        m[f"pu_full{l_}"] = p["peer_u"][l_]; m[f"pv_full{l_}"] = p["peer_v"][l_]
    m["skT"] = np.transpose(p["peer_subkeys"], (0, 1, 3, 2))
    m["w1k"] = p["cmp_w1_k"]; m["w1v"] = p["cmp_w1_v"]; m["w2k"] = p["cmp_w2_k"]; m["w2v"] = p["cmp_w2_v"]
    m["posk"] = p["cmp_pos_k"].reshape(DEPTH, 2048)
    m["posv"] = p["cmp_pos_v"].reshape(DEPTH, 2048)
    m["sink2"] = p["sinks"][:, headsB]
    rb = p["rel_bias"]
    m["b31c"] = rb[31:32, headsA]; m["b31s"] = rb[31:32, [16 + h for h in headsB]]
    ql = np.arange(128)[:, None]; xx = np.arange(NTAB)[None, :]
    dist_c = ql + 16 * (NTAB - 1 - xx) - 127
    m["tcr"] = np.stack([rb[_bucket(dist_c), h] for h in headsA], 0)
    m["mc"] = np.where(dist_c >= 0, 0.0, -30000.0)
    kk = np.arange(128)[:, None]; qq = np.arange(128)[None, :]
    m["tselr"] = np.stack([np.stack([rb[_bucket(128 * r + qq - kk), h] for h in headsA[:2]], 1) for r in range(NREL)], 0)
    m["tswar"] = np.stack([np.stack([rb[_bucket(128 * r + qq - kk), 16 + h] for h in headsB], 1) for r in range(2)], 0)
    m["m0"] = np.where(qq >= kk, 0.0, BIGNEG)
    m["m1"] = np.where(qq < kk, 0.0, BIGNEG)
    m["gexp"] = (np.arange(8192)[None, :] // 64 == np.arange(128)[:, None])
    m["ident"] = np.eye(128)
    m["iota16"] = np.tile(np.arange(16)[None, :], (128, 1))
    bon = np.zeros((128, 3), np.float32)
    bon[:64, 0] = 1e4; bon[:, 1] = 1e4; bon[:64, 2] = -1e30; bon[64:, 2] = 1e4
    m["bon"] = bon
    return {k: np.ascontiguousarray(np.asarray(v, dtype=np.float32)) for k, v in m.items()}


def run(cfg, p, stop_after=None):
    nc, nops = build_nc(cfg, stop_after)
    in_maps = [make_inputs(cfg, c, p) for c in range(N_CORES)]
    res = run_bass_kernel_spmd(nc, in_maps, core_ids=list(range(N_CORES)))
    T, NCH, CR, PR = cfg.T, cfg.NCH, cfg.CR, cfg.PR
    out = np.zeros((T, D), np.float32)
    for c in range(N_CORES):
        yl = res.results[c]["y"]
        for k in range(NCH):
            out[k * CR + c * PR:k * CR + (c + 1) * PR] = yl[k * PR:(k + 1) * PR]
    return out.reshape(cfg.B, cfg.S, D), res


def kernel(x, attn_norm, w_in, cmp_pos_k, cmp_w1_k, cmp_w2_k, cmp_pos_v, cmp_w1_v, cmp_w2_v,
           sinks, w_out, ffn_norm, peer_wq, peer_subkeys, peer_u, peer_v, rel_bias, final_norm):
    p = dict(x=x, attn_norm=attn_norm, w_in=w_in, cmp_pos_k=cmp_pos_k, cmp_w1_k=cmp_w1_k, cmp_w2_k=cmp_w2_k,
             cmp_pos_v=cmp_pos_v, cmp_w1_v=cmp_w1_v, cmp_w2_v=cmp_w2_v, sinks=sinks, w_out=w_out,
             ffn_norm=ffn_norm, peer_wq=peer_wq, peer_subkeys=peer_subkeys, peer_u=peer_u, peer_v=peer_v,
             rel_bias=rel_bias, final_norm=final_norm)
    p = {k: np.asarray(v, dtype=np.float32) for k, v in p.items()}
    B, S, _ = p["x"].shape
    cfg = Cfg(S=S, NKEYS=p["peer_subkeys"].shape[2], B=B)
    out, _ = run(cfg, p)
    return out.astype(np.float32)
```

```python
import math
import types
import numpy as np
import concourse.bass as bass
import concourse.mybir as mybir
from concourse.bass_utils import run_bass_kernel_spmd
from contextlib import ExitStack

F32 = mybir.dt.float32
BF16 = mybir.dt.bfloat16
I32 = mybir.dt.int32
U32 = mybir.dt.uint32
AF = mybir.ActivationFunctionType
ALU = mybir.AluOpType
AX = mybir.AxisListType

D = 2048
KC = 16
DEPTH = 2
N_CORES = 8
RMS_EPS = 1e-6
NTAB = 103
NREL = 13
BIGNEG = -240000.0
NFM = 15
NTM = 198
F_KC, F_VC, F_KS, F_KW, F_KB = 10, 11, 12, 13, 14


class Cfg:
    def __init__(self, S=16384, NKEYS=128, B=2):
        self.S, self.NKEYS, self.B = S, NKEYS, B
        self.T = B * S
        self.NT = S // 128
        self.NC = S // 16 - 1
        self.NCP = ((self.NC + 127) // 128) * 128
        self.NS = S // 64
        self.NJG = (self.NS + 127) // 128
        self.NCH = max(1, self.T // 4096)
        self.CR = self.T // self.NCH
        self.PR = self.CR // 8
        self.TOKL = self.T // 8
        self.NEXP = NKEYS * NKEYS
        self.ESH = self.NEXP // 8


class Seq:
    ENGS = ("tensor", "vector", "scalar", "gpsimd", "sync")

    def __init__(self):
        self.ops = []

    def add(self, eng, fn, kind="c"):
        if fn.__closure__:
            cells = tuple(types.CellType(c.cell_contents) for c in fn.__closure__)
            fn = types.FunctionType(fn.__code__, fn.__globals__, fn.__name__, fn.__defaults__, cells)
        self.ops.append((eng, fn, kind))

    def pe(self, fn): self.add("tensor", fn)
    def dve(self, fn): self.add("vector", fn)
    def act(self, fn): self.add("scalar", fn)
    def pool(self, fn): self.add("gpsimd", fn)
    def dma(self, fn): self.add("sync", fn, "d")
    def pdma(self, fn): self.add("gpsimd", fn, "d")
    def cc(self, fn): self.add("gpsimd", fn, "c")

    def emit(self, sems, block):
        plan = []
        cnt = {e: 0 for e in self.ENGS}
        prev = None
        for (eng, fn, kind) in self.ops:
            w = (prev, cnt[prev]) if prev is not None else None
            inc = 16 if kind == "d" else 1
            cnt[eng] += inc
            plan.append((eng, fn, w, inc))
            prev = eng
        final = (prev, cnt[prev]) if prev is not None else None

        def make(engname):
            def body(e):
                lastwait = {}
                for (eng, fn, w, inc) in plan:
                    if eng != engname:
                        continue
                    if w is not None and lastwait.get(w[0], -1) < w[1]:
                        e.wait_ge(sems[w[0]], w[1])
                        lastwait[w[0]] = w[1]
                    fn(e).then_inc(sems[eng], inc)
                if final is not None:
                    e.wait_ge(sems[final[0]], final[1])
            return body

        block.tensor(make("tensor"))
        block.vector(make("vector"))
        block.scalar(make("scalar"))
        block.gpsimd(make("gpsimd"))
        block.sync(make("sync"))


def build_nc(cfg, stop_after=None):
    S, T, NT, NC, NCP, NS, NJG = cfg.S, cfg.T, cfg.NT, cfg.NC, cfg.NCP, cfg.NS, cfg.NJG
    NCH, CR, PR, TOKL, NK, NEXP, ESH = cfg.NCH, cfg.CR, cfg.PR, cfg.TOKL, cfg.NKEYS, cfg.NEXP, cfg.ESH
    nc = bass.Bass("TRN2", target_bir_lowering=False)

    def din(name, shape, dt=F32):
        return nc.dram_tensor(name, list(shape), dt, kind="ExternalInput").ap()

    def dint(name, shape, dt=F32):
        return nc.dram_tensor(name, list(shape), dt)

    x_in = din("x", [TOKL, D])
    an_in = din("an", [DEPTH, D]); fn_in = din("fn", [DEPTH, D]); fin_in = din("fin", [1, D])
    wsel_in = din("wsel", [DEPTH, D, NFM * 64 + NTM])
    wo_in = din("wo", [DEPTH, 256, D])
    wq_in = din("wq_full", [DEPTH, D, D])
    pu_in = [din(f"pu_full{l_}", [NEXP, D]) for l_ in range(DEPTH)]
    pv_in = [din(f"pv_full{l_}", [NEXP, D]) for l_ in range(DEPTH)]
    skt_in = din("skT", [DEPTH, 2, 128, NK])
    w1_in = {"k": din("w1k", [DEPTH, 2048, 128]), "v": din("w1v", [DEPTH, 2048, 128])}
    w2_in = {"k": din("w2k", [DEPTH, 128, 64]), "v": din("w2v", [DEPTH, 128, 64])}
    pos_in = {"k": din("posk", [DEPTH, 2048]), "v": din("posv", [DEPTH, 2048])}
    sink_in = din("sink2", [DEPTH, 2])
    b31c_in = din("b31c", [1, 8]); b31s_in = din("b31s", [1, 2])
    tcr_in = din("tcr", [8, 128, NTAB]); mc_in = din("mc", [128, NTAB])
    tsel_in = din("tselr", [NREL, 128, 2, 128]); tswa_in = din("tswar", [2, 128, 2, 128])
    m0_in = din("m0", [128, 128]); m1_in = din("m1", [128, 128])
    gexp_in = din("gexp", [128, 8192]); ident_in = din("ident", [128, 128])
    iota_in = din("iota16", [128, 16]); bon_in = din("bon", [128, 3])
    y_out = nc.dram_tensor("y", [TOKL, D], F32, kind="ExternalOutput").ap()

    ikind = {"kind": "ExternalOutput"} if stop_after in ("proj", "attn") else {}
    BIN = dint("ag_in", [128, D]); STAGE = dint("ag_out", [N_CORES * 128, D])
    red = [dint(f"red{k}", [PR, D]) for k in range(NCH)]
    part = [nc.dram_tensor(f"part{k}", [CR, D], F32, **ikind) for k in range(NCH)]
    h_loc = [dint(f"h_loc{k}", [PR, D]) for k in range(NCH)]
    h_full = [dint(f"h_full{k}", [CR, D]) for k in range(NCH)]
    FMt = nc.dram_tensor("FM", [64, NFM, T], BF16, **ikind); FM = FMt.ap()
    TMt = nc.dram_tensor("TM", [T, 192], BF16, **ikind); TM = TMt.ap()
    GTt_ = nc.dram_tensor("GT", [T, 6], F32, **ikind); GT = GTt_.ap()
    XN = dint("XN", [128, D], BF16).ap()
    PUb = dint("PUb", [NEXP, D], BF16); PVb = dint("PVb", [NEXP, D], BF16)

    ALLR = [list(range(N_CORES))]
    s = Seq()

    def rows(lst, per, g0, n=128):
        k, off = divmod(g0, per)
        return lst[k].ap()[off:off + n, :]

    R0 = 128

    def gather_rows(src_fn, SH, dst):
        for j in range(SH // R0):
            s.dma(lambda e, j=j: e.dma_start(out=BIN.ap()[:, :], in_=src_fn(j)))
            s.cc(lambda e: e.collective_compute("AllGather", ALU.bypass, replica_groups=ALLR,
                                                ins=[BIN.ap().opt()], outs=[STAGE.ap().opt()]))
            s.dma(lambda e, j=j: e.dma_start(
                out=dst.ap().rearrange("(r s) d -> r s d", r=N_CORES)[:, j * R0:(j + 1) * R0, :],
                in_=STAGE.ap().rearrange("(r s) d -> r s d", r=N_CORES)))

    with ExitStack() as top:
        uid = [0]

        def SB(es, name, shape, dt=F32):
            uid[0] += 1
            return es.enter_context(nc.sbuf_tensor(f"{name}_{uid[0]}", list(shape), dt))

        def PS(es, name, shape, dt=F32):
            uid[0] += 1
            return es.enter_context(nc.psum_tensor(f"{name}_{uid[0]}", list(shape), dt))

        sems = {e: top.enter_context(nc.semaphore("s_" + e)) for e in Seq.ENGS}
        identf = SB(top, "identf", [128, 128]); identb = SB(top, "identb", [128, 128], BF16)
        st = SB(top, "st", [128, 16])
        s.dma(lambda e: e.dma_start(out=identf[:], in_=ident_in[:, :]))
        s.dve(lambda e: e.tensor_copy(out=identb[:], in_=identf[:]))
        for k in range(NCH):
            s.dma(lambda e, k=k: e.dma_start(out=h_loc[k].ap()[:, :], in_=x_in[k * PR:(k + 1) * PR, :]))

        def rmsnorm(src, g, dst, junk):
            s.act(lambda e: e.activation(out=junk[:], in_=src[:], func=AF.Square, accum_out=st[:, 0:1]))
            s.dve(lambda e: e.tensor_scalar(out=st[:, 1:2], in0=st[:, 0:1], scalar1=1.0 / D, scalar2=RMS_EPS,
                                            op0=ALU.mult, op1=ALU.add))
            s.act(lambda e: e.activation(out=st[:, 2:3], in_=st[:, 1:2], func=AF.Sqrt))
            s.dve(lambda e: e.reciprocal(out=st[:, 3:4], in_=st[:, 2:3]))
            s.dve(lambda e: e.scalar_tensor_tensor(out=dst[:], in0=src[:], scalar=st[:, 3:4], in1=g[:],
                                                   op0=ALU.mult, op1=ALU.mult))

        def gelu(src_ap, bias_ap, out_ap, xg, t1, np_, nf):
            if bias_ap is not None:
                s.act(lambda e: e.activation(out=xg[0:np_, 0:nf], in_=src_ap, func=AF.Identity, bias=bias_ap))
            else:
                s.act(lambda e: e.copy(out=xg[0:np_, 0:nf], in_=src_ap))
            s.dve(lambda e: e.tensor_tensor(out=t1[0:np_, 0:nf], in0=xg[0:np_, 0:nf], in1=xg[0:np_, 0:nf], op=ALU.mult))
            s.dve(lambda e: e.tensor_scalar(out=t1[0:np_, 0:nf], in0=t1[0:np_, 0:nf], scalar1=0.044715, scalar2=1.0,
                                            op0=ALU.mult, op1=ALU.add))
            s.dve(lambda e: e.tensor_tensor(out=t1[0:np_, 0:nf], in0=t1[0:np_, 0:nf], in1=xg[0:np_, 0:nf], op=ALU.mult))
            s.act(lambda e: e.activation(out=t1[0:np_, 0:nf], in_=t1[0:np_, 0:nf], func=AF.Tanh,
                                         scale=0.7978845608028654))
            s.dve(lambda e: e.scalar_tensor_tensor(out=t1[0:np_, 0:nf], in0=t1[0:np_, 0:nf], scalar=1.0,
                                                   in1=xg[0:np_, 0:nf], op0=ALU.add, op1=ALU.mult))
            s.act(lambda e: e.mul(out_ap, t1[0:np_, 0:nf], 0.5))

        def finish_early():
            with ExitStack() as es:
                bA = SB(es, "fe_bA", [128, D])
                for r0 in range(0, TOKL, 128):
                    kk, off = divmod(r0, PR)
                    s.dma(lambda e, kk=kk, off=off: e.dma_start(out=bA[:], in_=h_loc[kk].ap()[off:off + 128, :]))
                    s.dma(lambda e, r0=r0: e.dma_start(out=y_out[r0:r0 + 128, :], in_=bA[:]))

        stopped = False
        for l in range(DEPTH):
            if stopped:
                break
            last = (l == DEPTH - 1)
            for k in range(NCH):
                s.cc(lambda e, k=k: e.collective_compute("AllGather", ALU.bypass, replica_groups=ALLR,
                                                         ins=[h_loc[k].ap().opt()], outs=[h_full[k].ap().opt()]))

            with ExitStack() as es:
                WFb = SB(es, "WFb", [128, KC, NFM * 64], BF16); WTb = SB(es, "WTb", [128, KC, NTM], BF16)
                wst = SB(es, "wst", [128, NFM * 64 + NTM])
                G = SB(es, "G", [128, D]); xt = SB(es, "xt", [128, D]); sq = SB(es, "sq", [128, D])
                ub = SB(es, "ub", [128, D], BF16); uT = SB(es, "uT", [128, KC, 512], BF16)
                stf = SB(es, "stf", [64, 512], BF16); stv = SB(es, "stv", [128, 192], BF16); stg = SB(es, "stg", [128, 6])
                pt = PS(es, "pt", [128, 128], BF16); pf = PS(es, "pf", [128, 512]); pv = PS(es, "pv", [128, 512])
                for kc in range(KC):
                    s.dma(lambda e, kc=kc: e.dma_start(out=wst[:], in_=wsel_in[l, kc * 128:(kc + 1) * 128, :]))
                    s.dve(lambda e, kc=kc: e.tensor_copy(out=WFb[:, kc, :], in_=wst[:, 0:NFM * 64]))
                    s.dve(lambda e, kc=kc: e.tensor_copy(out=WTb[:, kc, :], in_=wst[:, NFM * 64:NFM * 64 + NTM]))
                s.dma(lambda e: e.dma_start(out=G[:], in_=an_in[l:l + 1, :].partition_broadcast(128)))
                for tg in range(T // 512):
                    for tt in range(4):
                        g0 = tg * 512 + tt * 128
                        s.dma(lambda e, g0=g0: e.dma_start(out=xt[:], in_=rows(h_full, CR, g0)))
                        rmsnorm(xt, G, ub, sq)
                        for kc in range(KC):
                            s.pe(lambda e, kc=kc: e.transpose(pt[:], ub[:, kc * 128:(kc + 1) * 128], identb[:]))
                            s.act(lambda e, kc=kc, tt=tt: e.copy(out=uT[:, kc, tt * 128:(tt + 1) * 128], in_=pt[:]))
                    for f in range(NFM):
                        for kc in range(KC):
                            s.pe(lambda e, kc=kc, f=f: e.matmul(pf[0:64, :], lhsT=WFb[:, kc, f * 64:(f + 1) * 64],
                                                               rhs=uT[:, kc, :], start=(kc == 0), stop=(kc == KC - 1)))
                        s.act(lambda e: e.copy(out=stf[:], in_=pf[0:64, :]))
                        s.dma(lambda e, f=f, tg=tg: e.dma_start(out=FM[:, f, tg * 512:(tg + 1) * 512], in_=stf[:]))
                    for tt in range(4):
                        g0 = tg * 512 + tt * 128
                        for kc in range(KC):
                            s.pe(lambda e, kc=kc, tt=tt: e.matmul(pv[:, 0:NTM], lhsT=uT[:, kc, tt * 128:(tt + 1) * 128],
                                                                 rhs=WTb[:, kc, :], start=(kc == 0), stop=(kc == KC - 1)))
                        s.act(lambda e: e.copy(out=stv[:], in_=pv[:, 0:192]))
                        s.act(lambda e: e.activation(out=stg[:], in_=pv[:, 192:198], func=AF.Sigmoid))
                        s.dma(lambda e, g0=g0: e.dma_start(out=TM[g0:g0 + 128, :], in_=stv[:]))
                        s.dma(lambda e, g0=g0: e.dma_start(out=GT[g0:g0 + 128, :], in_=stg[:]))
            if stop_after == "proj":
                finish_early(); stopped = True
                break

            with ExitStack() as lay:
                KCC = [SB(lay, f"KCC{b}", [64, NCP], BF16) for b in range(cfg.B)]
                VCC = [SB(lay, f"VCC{b}", [128, NCP // 128, 64], BF16) for b in range(cfg.B)]
                with ExitStack() as es:
                    w1s = SB(es, "w1s", [64, 32, 128]); W1b = SB(es, "W1b", [64, 32, 128], BF16)
                    W1c = SB(es, "W1c", [128, 16, 128]); posc = SB(es, "posc", [128, 16, 2])
                    w2s = SB(es, "w2s", [128, 64]); W2b = SB(es, "W2b", [128, 64], BF16)
                    PB = SB(es, "PB", [128, 2]); BIG = SB(es, "BIG", [64, S], BF16)
                    HID = SB(es, "HID", [128, NCP], BF16)
                    xg = SB(es, "xg", [128, 512]); t1 = SB(es, "t1", [128, 512])
                    ph = PS(es, "ph", [128, 512]); p2 = PS(es, "p2", [128, 512]); pb = PS(es, "pbb", [128, 512])
                    for which in ("k", "v"):
                        s.dma(lambda e, which=which: e.dma_start(
                            out=w1s[:], in_=w1_in[which][l].rearrange("(p d) h -> d p h", d=64)))
                        s.dve(lambda e: e.tensor_copy(out=W1b[:], in_=w1s[:]))
                        s.dma(lambda e, which=which: e.dma_start(
                            out=W1c[:], in_=w1_in[which][l].rearrange("(c q) h -> q c h", q=128)))
                        s.dve(lambda e: e.memset(posc[:], 0.0))
                        s.dma(lambda e, which=which: e.dma_start(
                            out=xg[:, 0:16], in_=pos_in[which][l].rearrange("(c q) -> q c", q=128),
                            allow_slow_non_contiguous=True))
                        s.dve(lambda e: e.tensor_copy(out=posc[:, :, 0], in_=xg[:, 0:16]))
                        for c in range(16):
                            s.pe(lambda e, c=c: e.matmul(pb[:, 0:2], lhsT=W1c[:, c, :], rhs=posc[:, c, :],
                                                         start=(c == 0), stop=(c == 15)))
                        s.act(lambda e: e.copy(out=PB[:], in_=pb[:, 0:2]))
                        s.dma(lambda e, which=which: e.dma_start(out=w2s[:], in_=w2_in[which][l]))
                        s.dve(lambda e: e.tensor_copy(out=W2b[:], in_=w2s[:]))
                        fidx = F_KC if which == "k" else F_VC
                        for b in range(cfg.B):
                            s.dma(lambda e, b=b, fidx=fidx: e.dma_start(out=BIG[:], in_=FM[:, fidx, b * S:(b + 1) * S]))
                            for n0 in range(0, NC, 512):
                                nn = min(512, NC - n0)
                                for p in range(32):
                                    s.pe(lambda e, p=p, n0=n0, nn=nn: e.matmul(
                                        ph[:, 0:nn], lhsT=W1b[:, p, :],
                                        rhs=BIG[:, 16 * n0 + p:16 * (n0 + nn - 1) + p + 1:16],
                                        start=(p == 0), stop=(p == 31)))
                                gelu(ph[:, 0:nn], PB[:, 0:1], HID[:, n0:n0 + nn], xg, t1, 128, nn)
                            if which == "k":
                                for n0 in range(0, NC, 512):
                                    nn = min(512, NC - n0)
                                    s.pe(lambda e, n0=n0, nn=nn: e.matmul(p2[0:64, 0:nn], lhsT=W2b[:], rhs=HID[:, n0:n0 + nn],
                                                                         start=True, stop=True))
                                    s.act(lambda e, n0=n0, nn=nn, b=b: e.copy(out=KCC[b][:, n0:n0 + nn], in_=p2[0:64, 0:nn]))
                            else:
                                for cc in range((NC + 127) // 128):
                                    w = min(128, NC - cc * 128)
                                    s.pe(lambda e, cc=cc, w=w: e.matmul(p2[0:w, 0:64], lhsT=HID[:, cc * 128:cc * 128 + w],
                                                                       rhs=W2b[:], start=True, stop=True))
                                    s.act(lambda e, cc=cc, w=w, b=b: e.copy(out=VCC[b][0:w, cc, :], in_=p2[0:w, 0:64]))

                with ExitStack() as es:
                    B31C = SB(es, "B31C", [128, 8]); B31S = SB(es, "B31S", [128, 2])
                    TcM = SB(es, "TcM", [128, 8, NTAB]); MC = SB(es, "MC", [128, NTAB])
                    stgT = SB(es, "stgT", [128, NREL, 2, 128])
                    TSEL = SB(es, "TSEL", [128, NREL, 2, 128], BF16); TWIN = SB(es, "TWIN", [128, 5, 2, 128], BF16)
                    TSWA = SB(es, "TSWA", [128, 2, 2, 128], BF16)
                    M0 = SB(es, "M0", [128, 128]); M1 = SB(es, "M1", [128, 128])
                    SINKE = SB(es, "SINKE", [128, 2]); BON = SB(es, "BON", [128, 3])
                    GEXP = SB(es, "GEXP", [128, 8192], BF16); gst = SB(es, "gst", [128, 2048])
                    WO = SB(es, "WO", [128, 2, D], BF16)
                    KS = SB(es, "KS", [64, S], BF16); VS = SB(es, "VS", [128, NT, 65], BF16)
                    QA = SB(es, "QA", [64, 8, 128], BF16); QB = SB(es, "QB", [64, 2, 128], BF16)
                    GTt = SB(es, "GTt", [128, 6])
                    KWt = SB(es, "KWt", [64, 640], BF16); VWt = SB(es, "VWt", [128, 5, 65], BF16)
                    KBt = SB(es, "KBt", [64, 256], BF16); VBt = SB(es, "VBt", [128, 2, 65], BF16)
                    E = SB(es, "E", [128, 1024]); Z = SB(es, "Z", [128, NTAB])
                    IMP = SB(es, "IMP", [128, 1040]); SC = SB(es, "SC", [128, 256]); SC2 = SB(es, "SC2", [128, 256])
                    M8 = SB(es, "M8", [128, 16]); SM = SB(es, "SM", [128, 8]); RS8 = SB(es, "RS8", [128, 8])
                    NEGSEL = SB(es, "NEGSEL", [128, NJG * 128]); NST = SB(es, "NST", [128, NJG, 2, 128], BF16)
                    ET = SB(es, "ET", [128, 128], BF16); PT = SB(es, "PT", [128, 256], BF16)
                    OCS = SB(es, "OCS", [128, 2, 64]); OSS = SB(es, "OSS", [128, 256]); OWS = SB(es, "OWS", [128, 256])
                    OBS = SB(es, "OBS", [128, 256]); OT = SB(es, "OT", [128, 256]); OTT = SB(es, "OTT", [128, 2, 128], BF16)
                    CF = SB(es, "CF", [128, 8]); STG = SB(es, "STG", [128, 512])
                    CL = PS(es, "CL", [128, 1024]); SL = PS(es, "SL", [128, 512]); OSp = PS(es, "OSp", [128, 512])
                    OWp = PS(es, "OWp", [128, 512]); OBp = PS(es, "OBp", [128, 512]); TP = PS(es, "TP", [128, 512])
                    OC = PS(es, "OC", [128, 512])

                    s.dma(lambda e: e.dma_start(out=B31C[:], in_=b31c_in[0:1, :].partition_broadcast(128)))
                    s.dma(lambda e: e.dma_start(out=B31S[:], in_=b31s_in[0:1, :].partition_broadcast(128)))
                    s.dma(lambda e: e.dma_start(out=TcM[:], in_=tcr_in.rearrange("h q x -> q h x")))
                    s.dma(lambda e: e.dma_start(out=MC[:], in_=mc_in[:, :]))
                    s.dma(lambda e: e.dma_start(out=M0[:], in_=m0_in[:, :]))
                    s.dma(lambda e: e.dma_start(out=M1[:], in_=m1_in[:, :]))
                    s.dma(lambda e: e.dma_start(out=BON[:], in_=bon_in[:, :]))
                    for h in range(8):
                        s.dve(lambda e, h=h: e.scalar_tensor_tensor(out=TcM[:, h, :], in0=TcM[:, h, :], scalar=B31C[:, h:h + 1],
                                                                    in1=MC[:], op0=ALU.subtract, op1=ALU.add))
                    s.dma(lambda e: e.dma_start(out=stgT[:], in_=tsel_in.rearrange("r k h q -> k r h q")))
                    for h in range(2):
                        s.dve(lambda e, h=h: e.tensor_scalar(out=stgT[:, :, h, :], in0=stgT[:, :, h, :], scalar1=B31C[:, h:h + 1],
                                                             scalar2=8.0, op0=ALU.subtract, op1=ALU.mult))
                        s.dve(lambda e, h=h: e.tensor_tensor(out=stgT[:, 0, h, :], in0=stgT[:, 0, h, :], in1=M0[:], op=ALU.add))
                    s.dve(lambda e: e.tensor_copy(out=TSEL[:], in_=stgT[:]))
                    for h in range(2):
                        s.dve(lambda e, h=h: e.tensor_tensor(out=stgT[:, 4, h, :], in0=stgT[:, 4, h, :], in1=M1[:], op=ALU.add))
                    s.dve(lambda e: e.tensor_copy(out=TWIN[:], in_=stgT[:, 0:5, :, :]))
                    s.dma(lambda e: e.dma_start(out=stgT[:, 0:2, :, :], in_=tswa_in.rearrange("r k h q -> k r h q")))
                    for h in range(2):
                        s.dve(lambda e, h=h: e.tensor_scalar(out=stgT[:, 0:2, h, :], in0=stgT[:, 0:2, h, :], scalar1=B31S[:, h:h + 1],
                                                             scalar2=8.0, op0=ALU.subtract, op1=ALU.mult))
                        s.dve(lambda e, h=h: e.tensor_tensor(out=stgT[:, 0, h, :], in0=stgT[:, 0, h, :], in1=M0[:], op=ALU.add))
                        s.dve(lambda e, h=h: e.tensor_tensor(out=stgT[:, 1, h, :], in0=stgT[:, 1, h, :], in1=M1[:], op=ALU.add))
                    s.dve(lambda e: e.tensor_copy(out=TSWA[:], in_=stgT[:, 0:2, :, :]))
                    s.dma(lambda e: e.dma_start(out=SINKE[:], in_=sink_in[l:l + 1, :].partition_broadcast(128)))
                    s.dve(lambda e: e.tensor_tensor(out=SINKE[:], in0=SINKE[:], in1=B31S[:], op=ALU.subtract))
                    s.act(lambda e: e.activation(out=SINKE[:], in_=SINKE[:], func=AF.Exp))
                    for q4 in range(4):
                        s.dma(lambda e, q4=q4: e.dma_start(out=gst[:], in_=gexp_in[:, q4 * 2048:(q4 + 1) * 2048]))
                        s.dve(lambda e, q4=q4: e.tensor_copy(out=GEXP[:, q4 * 2048:(q4 + 1) * 2048], in_=gst[:]))
                    for c in range(2):
                        s.dma(lambda e, c=c: e.dma_start(out=gst[:], in_=wo_in[l, c * 128:(c + 1) * 128, :]))
                        s.dve(lambda e, c=c: e.tensor_copy(out=WO[:, c, :], in_=gst[:]))
                    s.dve(lambda e: e.memset(VS[:], 1.0))
                    s.dve(lambda e: e.memset(VWt[:], 1.0))
                    s.dve(lambda e: e.memset(VBt[:], 1.0))

                    def band_attn(i, lo, Kt, Vt, Qt, TAB, Op, OSB):
                        nch = i - lo + 1
                        for idx in range(nch):
                            rel = i - (lo + idx)
                            s.pe(lambda e, idx=idx: e.matmul(SL[:, 0:256], lhsT=Kt[:, idx * 128:(idx + 1) * 128], rhs=Qt,
                                                             start=True, stop=False))
                            s.pe(lambda e, rel=rel: e.matmul(SL[:, 0:256], lhsT=identb[:], rhs=TAB[:, rel, :, :],
                                                             start=False, stop=True))
                            s.act(lambda e: e.activation(out=PT[:], in_=SL[:, 0:256], func=AF.Exp, scale=0.125))
                            for h in range(2):
                                s.pe(lambda e, idx=idx, h=h: e.matmul(
                                    Op[:, h * 128:h * 128 + 65], lhsT=PT[:, h * 128:(h + 1) * 128], rhs=Vt[:, idx, :],
                                    start=(idx == 0 and h == 0), stop=(idx == nch - 1 and h == 1), skip_group_check=True))
                        s.act(lambda e: e.copy(out=OSB[:], in_=Op[:, 0:256]))

                    for b in range(cfg.B):
                        s.dma(lambda e, b=b: e.dma_start(out=KS[:], in_=FM[:, F_KS, b * S:(b + 1) * S]))
                        for i0 in range(0, NT, 16):
                            i1 = min(NT, i0 + 16)
                            s.dma(lambda e, b=b, i0=i0, i1=i1: e.dma_start(
                                out=VS[:, i0:i1, 0:64],
                                in_=TM[b * S + i0 * 128:b * S + i1 * 128, 0:64].rearrange("(i p) d -> p i d", p=128)))
                        for i in range(NT):
                            t0 = b * S + i * 128
                            lo = max(0, i - 4); lob = max(0, i - 1)
                            s.dma(lambda e, t0=t0: e.dma_start(out=QA[:], in_=FM[:, 0:8, t0:t0 + 128]))
                            s.dma(lambda e, t0=t0: e.dma_start(out=QB[:], in_=FM[:, 8:10, t0:t0 + 128]))
                            s.dma(lambda e, t0=t0: e.dma_start(out=GTt[:], in_=GT[t0:t0 + 128, :]))
                            nw = i - lo + 1
                            s.dma(lambda e, b=b, lo=lo, nw=nw: e.dma_start(
                                out=KWt[:, 0:nw * 128], in_=FM[:, F_KW, b * S + lo * 128:b * S + (lo + nw) * 128]))
                            s.dma(lambda e, b=b, lo=lo, nw=nw: e.dma_start(
                                out=VWt[:, 0:nw, 0:64],
                                in_=TM[b * S + lo * 128:b * S + (lo + nw) * 128, 64:128].rearrange("(c p) d -> p c d", p=128)))
                            nb = i - lob + 1
                            s.dma(lambda e, b=b, lob=lob, nb=nb: e.dma_start(
                                out=KBt[:, 0:nb * 128], in_=FM[:, F_KB, b * S + lob * 128:b * S + (lob + nb) * 128]))
                            s.dma(lambda e, b=b, lob=lob, nb=nb: e.dma_start(
                                out=VBt[:, 0:nb, 0:64],
                                in_=TM[b * S + lob * 128:b * S + (lob + nb) * 128, 128:192].rearrange("(c p) d -> p c d", p=128)))

                            n = 8 * i + 7
                            c_lo = max(0, 8 * i - 96)
                            x_lo = c_lo - (8 * i - 96)
                            nnear = n - c_lo
                            s.dve(lambda e: e.memset(IMP[:], 0.0))
                            for h in range(8):
                                for n0 in range(0, n, 512):
                                    nn = min(512, n - n0)
                                    s.pe(lambda e, h=h, n0=n0, nn=nn, b=b: e.matmul(
                                        CL[:, n0:n0 + nn], lhsT=QA[:, h, :], rhs=KCC[b][:, n0:n0 + nn], start=True, stop=True))
                                for (a0, a1) in ([(c_lo, n)] if (c_lo >= 512 or n <= 512) else [(c_lo, 512), (512, n)]):
                                    s.dve(lambda e, h=h, c_lo=c_lo, a0=a0, a1=a1, x_lo=x_lo: e.scalar_tensor_tensor(
                                        out=Z[:, a0 - c_lo:a1 - c_lo], in0=CL[:, a0:a1], scalar=0.125,
                                        in1=TcM[:, h, x_lo + a0 - c_lo:x_lo + a1 - c_lo], op0=ALU.mult, op1=ALU.add))
                                s.act(lambda e, c_lo=c_lo, n=n, nnear=nnear: e.activation(
                                    out=E[:, c_lo:n], in_=Z[:, 0:nnear], func=AF.Exp, accum_out=SM[:, 0:1]))
                                for (a0, a1) in ([(0, c_lo)] if c_lo <= 512 else [(0, 512), (512, c_lo)]):
                                    if a1 > a0:
                                        s.act(lambda e, a0=a0, a1=a1: e.activation(out=E[:, a0:a1], in_=CL[:, a0:a1], func=AF.Exp,
                                                                                   scale=0.125, accum_out=SM[:, 1:2]))
                                        s.dve(lambda e: e.tensor_tensor(out=SM[:, 0:1], in0=SM[:, 0:1], in1=SM[:, 1:2], op=ALU.add))
                                s.dve(lambda e: e.tensor_scalar(out=SM[:, 2:3], in0=SM[:, 0:1], scalar1=1e-30, scalar2=None,
                                                                op0=ALU.max))
                                s.dve(lambda e, h=h: e.reciprocal(out=RS8[:, h:h + 1], in_=SM[:, 2:3]))
                                if h == 0:
                                    s.dve(lambda e, h=h, n=n: e.tensor_scalar(out=IMP[:, 1:1 + n], in0=E[:, 0:n],
                                                                              scalar1=RS8[:, h:h + 1], scalar2=None, op0=ALU.mult))
                                else:
                                    s.dve(lambda e, h=h, n=n: e.scalar_tensor_tensor(
                                        out=IMP[:, 1:1 + n], in0=E[:, 0:n], scalar=RS8[:, h:h + 1], in1=IMP[:, 1:1 + n],
                                        op0=ALU.mult, op1=ALU.add))
                                if h < 2:
                                    ncc = (n + 127) // 128
                                    for cc in range(ncc):
                                        w = min(128, n - cc * 128)
                                        s.pe(lambda e, cc=cc, w=w: e.transpose(TP[0:w, 0:128], E[:, cc * 128:cc * 128 + w], identf[:]))
                                        s.act(lambda e, w=w: e.copy(out=ET[0:w, :], in_=TP[0:w, 0:128]))
                                        s.pe(lambda e, cc=cc, w=w, b=b, ncc=ncc: e.matmul(
                                            OC[:, 0:64], lhsT=ET[0:w, :], rhs=VCC[b][0:w, cc, :],
                                            start=(cc == 0), stop=(cc == ncc - 1)))
                                    s.dve(lambda e, h=h: e.tensor_scalar(out=OCS[:, h, :], in0=OC[:, 0:64], scalar1=RS8[:, h:h + 1],
                                                                         scalar2=None, op0=ALU.mult))

                            W = 2 * i + 2
                            s.dve(lambda e: e.memset(NEGSEL[:], BIGNEG))
                            if W <= 16:
                                s.dve(lambda e, W=W: e.memset(NEGSEL[:, 0:W], 0.0))
                            else:
                                s.dve(lambda e, W=W: e.tensor_copy(out=SC[:, 0:W], in_=IMP[:, 0:4 * W:4]))
                                for r_, wgt in ((1, 2.0), (2, 2.0), (3, 2.0), (4, 1.0)):
                                    s.dve(lambda e, W=W, r_=r_, wgt=wgt: e.scalar_tensor_tensor(
                                        out=SC[:, 0:W], in0=IMP[:, r_:r_ + 4 * W:4], scalar=wgt, in1=SC[:, 0:W],
                                        op0=ALU.mult, op1=ALU.add))
                                s.dve(lambda e: e.tensor_scalar(out=SC[:, 0:1], in0=SC[:, 0:1], scalar1=1e4, scalar2=None, op0=ALU.add))
                                s.dve(lambda e, i=i: e.tensor_tensor(out=SC[:, 2 * i - 1:2 * i + 2], in0=SC[:, 2 * i - 1:2 * i + 2],
                                                                     in1=BON[:], op=ALU.add))
                                s.dve(lambda e, W=W: e.max(out=M8[:, 0:8], in_=SC[:, 0:W]))
                                s.dve(lambda e, W=W: e.match_replace(out=SC2[:, 0:W], in_to_replace=M8[:, 0:8], in_values=SC[:, 0:W],
                                                                     imm_value=-3.0e38))
                                s.dve(lambda e, W=W: e.max(out=M8[:, 8:16], in_=SC2[:, 0:W]))
                                s.dve(lambda e, W=W: e.tensor_scalar(out=SC2[:, 0:W], in0=SC[:, 0:W], scalar1=M8[:, 15:16],
                                                                     scalar2=-BIGNEG, op0=ALU.is_ge, op1=ALU.mult))
                                s.dve(lambda e, W=W: e.tensor_scalar(out=NEGSEL[:, 0:W], in0=SC2[:, 0:W], scalar1=BIGNEG, scalar2=None,
                                                                     op0=ALU.add))
                            for jg in range(NJG):
                                s.pe(lambda e, jg=jg: e.transpose(TP[:, 0:128], NEGSEL[:, jg * 128:(jg + 1) * 128], identf[:]))
                                for h in range(2):
                                    s.act(lambda e, jg=jg, h=h: e.copy(out=NST[:, jg, h, :], in_=TP[:, 0:128]))

                            for ck in range(i + 1):
                                rel = i - ck
                                s.pe(lambda e, ck=ck: e.matmul(SL[:, 0:256], lhsT=KS[:, ck * 128:(ck + 1) * 128], rhs=QA[:, 0:2, :],
                                                               start=True, stop=False))
                                s.pe(lambda e, ck=ck, rel=rel: e.matmul(
                                    SL[:, 0:256], lhsT=GEXP[:, 128 * (ck % 64):128 * (ck % 64) + 128], rhs=NST[:, ck // 64, :, :],
                                    start=False, stop=(rel >= NREL)))
                                if rel < NREL:
                                    s.pe(lambda e, rel=rel: e.matmul(SL[:, 0:256], lhsT=identb[:], rhs=TSEL[:, rel, :, :],
                                                                     start=False, stop=True))
                                s.act(lambda e: e.activation(out=PT[:], in_=SL[:, 0:256], func=AF.Exp, scale=0.125))
                                for h in range(2):
                                    s.pe(lambda e, ck=ck, h=h, i=i: e.matmul(
                                        OSp[:, h * 128:h * 128 + 65], lhsT=PT[:, h * 128:(h + 1) * 128], rhs=VS[:, ck, :],
                                        start=(ck == 0 and h == 0), stop=(ck == i and h == 1), skip_group_check=True))
                            s.act(lambda e: e.copy(out=OSS[:], in_=OSp[:, 0:256]))

                            band_attn(i, lo, KWt, VWt, QA[:, 0:2, :], TWIN, OWp, OWS)
                            band_attn(i, lob, KBt, VBt, QB[:, 0:2, :], TSWA, OBp, OBS)

                            for h in range(2):
                                s.dve(lambda e, h=h: e.reciprocal(out=CF[:, 0:1], in_=OSS[:, h * 128 + 64:h * 128 + 65]))
                                s.dve(lambda e, h=h: e.reciprocal(out=CF[:, 1:2], in_=OWS[:, h * 128 + 64:h * 128 + 65]))
                                s.dve(lambda e, h=h: e.tensor_tensor(out=CF[:, 0:1], in0=CF[:, 0:1], in1=GTt[:, 3 * h + 1:3 * h + 2], op=ALU.mult))
                                s.dve(lambda e, h=h: e.tensor_tensor(out=CF[:, 1:2], in0=CF[:, 1:2], in1=GTt[:, 3 * h + 2:3 * h + 3], op=ALU.mult))
                                s.dve(lambda e, h=h: e.tensor_scalar(out=OT[:, h * 64:(h + 1) * 64], in0=OCS[:, h, :],
                                                                     scalar1=GTt[:, 3 * h:3 * h + 1], scalar2=None, op0=ALU.mult))
                                s.dve(lambda e, h=h: e.scalar_tensor_tensor(
                                    out=OT[:, h * 64:(h + 1) * 64], in0=OSS[:, h * 128:h * 128 + 64], scalar=CF[:, 0:1],
                                    in1=OT[:, h * 64:(h + 1) * 64], op0=ALU.mult, op1=ALU.add))
                                s.dve(lambda e, h=h: e.scalar_tensor_tensor(
                                    out=OT[:, h * 64:(h + 1) * 64], in0=OWS[:, h * 128:h * 128 + 64], scalar=CF[:, 1:2],
                                    in1=OT[:, h * 64:(h + 1) * 64], op0=ALU.mult, op1=ALU.add))
                                s.dve(lambda e, h=h: e.tensor_tensor(out=CF[:, 2:3], in0=OBS[:, h * 128 + 64:h * 128 + 65],
                                                                     in1=SINKE[:, h:h + 1], op=ALU.add))
                                s.dve(lambda e: e.reciprocal(out=CF[:, 3:4], in_=CF[:, 2:3]))
                                s.dve(lambda e, h=h: e.tensor_scalar(out=OT[:, 128 + h * 64:128 + (h + 1) * 64],
                                                                     in0=OBS[:, h * 128:h * 128 + 64], scalar1=CF[:, 3:4],
                                                                     scalar2=None, op0=ALU.mult))
                            for c in range(2):
                                s.pe(lambda e, c=c: e.transpose(TP[:, 0:128], OT[:, c * 128:(c + 1) * 128], identf[:]))
                                s.act(lambda e, c=c: e.copy(out=OTT[:, c, :], in_=TP[:, 0:128]))
                            for cc in range(4):
                                for c in range(2):
                                    s.pe(lambda e, c=c, cc=cc: e.matmul(SL[:, 0:512], lhsT=OTT[:, c, :], rhs=WO[:, c, cc * 512:(cc + 1) * 512],
                                                                       start=(c == 0), stop=(c == 1)))
                                s.act(lambda e: e.copy(out=STG[:], in_=SL[:, 0:512]))
                                s.dma(lambda e, t0=t0, cc=cc: e.dma_start(out=rows(part, CR, t0)[:, cc * 512:(cc + 1) * 512], in_=STG[:]))
            if stop_after == "attn":
                finish_early(); stopped = True
                break

            for k in range(NCH):
                s.cc(lambda e, k=k: e.collective_compute("ReduceScatter", ALU.add, replica_groups=ALLR,
                                                         ins=[part[k].ap().opt()], outs=[red[k].ap().opt()]))

            if stop_after == "rs":
                with ExitStack() as es:
                    bA = SB(es, "rs_bA", [128, D])
                    for r0 in range(0, TOKL, 128):
                        kk, off = divmod(r0, PR)
                        s.dma(lambda e, kk=kk, off=off: e.dma_start(out=bA[:], in_=red[kk].ap()[off:off + 128, :]))
                        s.dma(lambda e, r0=r0: e.dma_start(out=y_out[r0:r0 + 128, :], in_=bA[:]))
                stopped = True
                break
            with ExitStack() as es:
                WQb = SB(es, "WQb", [128, KC, D], BF16)
                bA = SB(es, "bA", [128, D]); bB = SB(es, "bB", [128, D]); bC = SB(es, "bC", [128, D])
                bD = SB(es, "bD", [128, D]); bE = SB(es, "bE", [128, D])
                G2 = SB(es, "G2", [128, D]); GF = SB(es, "GF", [128, D])
                xnb = SB(es, "xnb", [128, D], BF16); xT = SB(es, "xT", [128, KC, 128], BF16)
                QT = SB(es, "QT", [128, KC, 128]); SKT = SB(es, "SKT", [128, 2, NK])
                S12 = SB(es, "S12", [128, NK]); S12b = SB(es, "S12b", [128, NK])
                V12 = SB(es, "V12", [128, 2, 16]); I12 = SB(es, "I12", [128, 2, 16]); IU = SB(es, "IU", [128, 16], U32)
                IU2 = SB(es, "IU2", [128, 16], U32)
                CAND = SB(es, "CAND", [128, 256]); CAND2 = SB(es, "CAND2", [128, 256]); SCV = SB(es, "SCV", [128, 16])
                AFt = SB(es, "AFt", [128, 16]); BFt = SB(es, "BFt", [128, 16]); OH = SB(es, "OH", [128, 16, 16])
                E1 = SB(es, "E1", [128, 16]); E2 = SB(es, "E2", [128, 16]); IOT = SB(es, "IOT", [128, 16])
                IDX = SB(es, "IDX", [128, 128]); GATE = SB(es, "GATE", [128, 128]); GE = SB(es, "GE", [128, 16])
                IDXT = SB(es, "IDXT", [128, 128], I32); GATET = SB(es, "GATET", [128, 128])
                HP = SB(es, "HP", [128, 128]); GH = SB(es, "GH", [128, 128]); xg = SB(es, "xg2", [128, 128]); t1 = SB(es, "t12", [128, 128])
                BB = SB(es, "BB", [128, 256], BF16)
                UGb = SB(es, "UGb", [128, D], BF16); VGb = SB(es, "VGb", [128, D], BF16)
                PO = PS(es, "PO", [128, 2048]); PQ = PS(es, "PQ", [128, 512]); TPf = PS(es, "TPf", [128, 512])
                PSs = PS(es, "PSs", [128, 512]); ptb = PS(es, "ptb", [128, 128], BF16)
                for kc in range(KC):
                    s.dma(lambda e, kc=kc: e.dma_start(out=bE[:], in_=wq_in[l, kc * 128:(kc + 1) * 128, :]))
                    s.dve(lambda e, kc=kc: e.tensor_copy(out=WQb[:, kc, :], in_=bE[:]))
                s.dma(lambda e: e.dma_start(out=SKT[:], in_=skt_in[l].rearrange("a d n -> d a n")))
                s.dma(lambda e: e.dma_start(out=G2[:], in_=fn_in[l:l + 1, :].partition_broadcast(128)))
                s.dma(lambda e: e.dma_start(out=GF[:], in_=fin_in[0:1, :].partition_broadcast(128)))
                s.dma(lambda e: e.dma_start(out=IOT[:], in_=iota_in[:, :]))
                s.dve(lambda e: e.memset(BB[:], 0.0))
                for (src_t, dst_t) in ((pu_in[l], PUb), (pv_in[l], PVb)):
                    for r0_ in range(0, NEXP, 128):
                        s.dma(lambda e, src_t=src_t, r0_=r0_: e.dma_start(out=bE[:], in_=src_t[r0_:r0_ + 128, :]))
                        s.dve(lambda e: e.tensor_copy(out=UGb[:], in_=bE[:]))
                        s.dma(lambda e, dst_t=dst_t, r0_=r0_: e.dma_start(out=dst_t.ap()[r0_:r0_ + 128, :], in_=UGb[:]))

                def top16(src, src2, n_, vout, iout_u):
                    s.dve(lambda e: e.max(out=vout[:, 0:8], in_=src[:, 0:n_]))
                    s.dve(lambda e: e.max_index(out=iout_u[:, 0:8], in_max=vout[:, 0:8], in_values=src[:, 0:n_]))
                    s.dve(lambda e: e.match_replace(out=src2[:, 0:n_], in_to_replace=vout[:, 0:8], in_values=src[:, 0:n_],
                                                    imm_value=-3.0e38))
                    s.dve(lambda e: e.max(out=vout[:, 8:16], in_=src2[:, 0:n_]))
                    s.dve(lambda e: e.max_index(out=iout_u[:, 8:16], in_max=vout[:, 8:16], in_values=src2[:, 0:n_]))

                for lt in range(TOKL // 128):
                    r0 = lt * 128
                    kk, off = divmod(r0, PR)
                    s.dma(lambda e, kk=kk, off=off: e.dma_start(out=bA[:], in_=h_loc[kk].ap()[off:off + 128, :]))
                    s.dma(lambda e, kk=kk, off=off: e.dma_start(out=bB[:], in_=red[kk].ap()[off:off + 128, :]))
                    s.dve(lambda e: e.tensor_tensor(out=bA[:], in0=bA[:], in1=bB[:], op=ALU.add))
                    rmsnorm(bA, G2, bC, bE)
                    s.dve(lambda e: e.tensor_copy(out=xnb[:], in_=bC[:]))
                    s.dma(lambda e: e.dma_start(out=XN[:, :], in_=xnb[:]))
                    for kc in range(KC):
                        s.pe(lambda e, kc=kc: e.transpose(ptb[:], xnb[:, kc * 128:(kc + 1) * 128], identb[:]))
                        s.act(lambda e, kc=kc: e.copy(out=xT[:, kc, :], in_=ptb[:]))
                    for cc in range(4):
                        for kc in range(KC):
                            s.pe(lambda e, kc=kc, cc=cc: e.matmul(PQ[:, 0:512], lhsT=xT[:, kc, :], rhs=WQb[:, kc, cc * 512:(cc + 1) * 512],
                                                                 start=(kc == 0), stop=(kc == KC - 1)))
                        s.act(lambda e, cc=cc: e.copy(out=bD[:, cc * 512:(cc + 1) * 512], in_=PQ[:, 0:512]))
                    for c in range(KC):
                        s.pe(lambda e, c=c: e.transpose(TPf[:, 0:128], bD[:, c * 128:(c + 1) * 128], identf[:]))
                        s.act(lambda e, c=c: e.copy(out=QT[:, c, :], in_=TPf[:, 0:128]))
                    for h in range(8):
                        for a in range(2):
                            s.pe(lambda e, h=h, a=a: e.matmul(PSs[:, 0:NK], lhsT=QT[:, 2 * h + a, :], rhs=SKT[:, a, :],
                                                              start=True, stop=True))
                            s.act(lambda e: e.copy(out=S12[:], in_=PSs[:, 0:NK]))
                            top16(S12, S12b, NK, V12[:, a, :], IU)
                            s.dve(lambda e, a=a: e.tensor_copy(out=I12[:, a, :], in_=IU[:]))
                        s.dve(lambda e: e.tensor_tensor(
                            out=CAND[:].rearrange("p (a b) -> p a b", a=16),
                            in0=V12[:, 0, :].unsqueeze(2).to_broadcast([128, 16, 16]),
                            in1=V12[:, 1, :].unsqueeze(1).to_broadcast([128, 16, 16]), op=ALU.add))
                        top16(CAND, CAND2, 256, SCV, IU2)
                        s.dve(lambda e: e.tensor_single_scalar(out=IU[:], in_=IU2[:], scalar=4, op=ALU.logical_shift_right))
                        s.dve(lambda e: e.tensor_copy(out=AFt[:], in_=IU[:]))
                        s.dve(lambda e: e.tensor_single_scalar(out=IU[:], in_=IU2[:], scalar=15, op=ALU.bitwise_and))
                        s.dve(lambda e: e.tensor_copy(out=BFt[:], in_=IU[:]))
                        for (sel_f, a, dst) in ((AFt, 0, E1), (BFt, 1, E2)):
                            s.dve(lambda e, sel_f=sel_f: e.tensor_tensor(
                                out=OH[:], in0=IOT[:].unsqueeze(1).to_broadcast([128, 16, 16]),
                                in1=sel_f[:].unsqueeze(2).to_broadcast([128, 16, 16]), op=ALU.is_equal))
                            s.dve(lambda e, a=a: e.tensor_tensor(
                                out=OH[:], in0=OH[:], in1=I12[:, a, :].unsqueeze(1).to_broadcast([128, 16, 16]), op=ALU.mult))
                            s.dve(lambda e, dst=dst: e.tensor_reduce(out=dst[:], in_=OH[:], axis=AX.X, op=ALU.add))
                        s.dve(lambda e, h=h: e.scalar_tensor_tensor(out=IDX[:, h * 16:(h + 1) * 16], in0=E1[:], scalar=float(NK),
                                                                    in1=E2[:], op0=ALU.mult, op1=ALU.add))
                        s.dve(lambda e: e.tensor_scalar(out=st[:, 8:9], in0=SCV[:, 0:1], scalar1=-1.0, scalar2=None, op0=ALU.mult))
                        s.act(lambda e: e.activation(out=GE[:], in_=SCV[:], func=AF.Exp, bias=st[:, 8:9], accum_out=st[:, 9:10]))
                        s.dve(lambda e: e.reciprocal(out=st[:, 10:11], in_=st[:, 9:10]))
                        s.dve(lambda e, h=h: e.tensor_scalar(out=GATE[:, h * 16:(h + 1) * 16], in0=GE[:], scalar1=st[:, 10:11],
                                                             scalar2=None, op0=ALU.mult))
                    s.pe(lambda e: e.transpose(TPf[:, 0:128], IDX[:], identf[:]))
                    s.act(lambda e: e.copy(out=GH[:], in_=TPf[:, 0:128]))
                    s.dve(lambda e: e.tensor_copy(out=IDXT[:], in_=GH[:]))
                    s.pe(lambda e: e.transpose(TPf[:, 0:128], GATE[:], identf[:]))
                    s.act(lambda e: e.copy(out=GATET[:], in_=TPf[:, 0:128]))
                    if stop_after == "peer_idx":
                        s.dma(lambda e: e.dma_start(out=y_out[0:128, 0:128], in_=IDX[:]))
                        s.dma(lambda e: e.dma_start(out=y_out[0:128, 128:256], in_=GATE[:]))
                        s.dma(lambda e: e.dma_start(out=y_out[0:128, 256:384], in_=GATET[:]))
                        s.dma(lambda e: e.dma_start(out=y_out[128:256, :], in_=bD[:]))
                        s.dma(lambda e: e.dma_start(out=y_out[256:384, :], in_=bB[:]))
                        s.dma(lambda e: e.dma_start(out=y_out[384:512, :], in_=bA[:]))
                        s.dma(lambda e: e.dma_start(out=y_out[0:128, 384:384 + 32], in_=S12[:]))
                        s.dma(lambda e: e.dma_start(out=y_out[0:128, 512:512 + 128], in_=QT[:, 15, :]))
                        break
                    npt = 2 if stop_after == "peer_g2" else 128
                    for t in range(npt):
                        s.pdma(lambda e, t=t: e.indirect_dma_start(
                            out=UGb[:], out_offset=None, in_=PUb.ap()[:, :],
                            in_offset=bass.IndirectOffsetOnAxis(ap=IDXT[:, t:t + 1].bitcast(U32), axis=0)))
                        s.dma(lambda e, t=t: e.dma_start(out=VGb[:], in_=XN[t:t + 1, :].partition_broadcast(128)))
                        s.dve(lambda e, t=t: e.scalar_tensor_tensor(out=bE[:], in0=UGb[:], scalar=1.0, in1=VGb[:],
                                                                    op0=ALU.mult, op1=ALU.mult, accum_out=HP[:, t:t + 1]))
                    if stop_after == "peer_g2":
                        s.dma(lambda e: e.dma_start(out=y_out[0:128, 0:128], in_=HP[:]))
                        break
                    gelu(HP[:], None, GH[:], xg, t1, 128, 128)
                    s.dve(lambda e: e.tensor_tensor(out=GH[:], in0=GH[:], in1=GATET[:], op=ALU.mult))
                    for t in range(128):
                        s.pdma(lambda e, t=t: e.indirect_dma_start(
                            out=VGb[:], out_offset=None, in_=PVb.ap()[:, :],
                            in_offset=bass.IndirectOffsetOnAxis(ap=IDXT[:, t:t + 1].bitcast(U32), axis=0)))
                        s.act(lambda e, t=t: e.copy(out=BB[:, 127:128], in_=GH[:, t:t + 1]))
                        for cc in range(4):
                            s.pe(lambda e, t=t, cc=cc: e.matmul(PO[:, cc * 512:(cc + 1) * 512], lhsT=BB[:, 127 - t:255 - t],
                                                               rhs=VGb[:, cc * 512:(cc + 1) * 512], start=(t == 0), stop=(t == 127)))
                    s.dve(lambda e: e.tensor_tensor(out=bA[:], in0=bA[:], in1=PO[:], op=ALU.add))
                    if not last and stop_after != "layer0":
                        s.dma(lambda e, kk=kk, off=off: e.dma_start(out=h_loc[kk].ap()[off:off + 128, :], in_=bA[:]))
                    elif stop_after == "layer0":
                        s.dma(lambda e, r0=r0: e.dma_start(out=y_out[r0:r0 + 128, :], in_=bA[:]))
                    else:
                        rmsnorm(bA, GF, bC, bE)
                        s.dma(lambda e, r0=r0: e.dma_start(out=y_out[r0:r0 + 128, :], in_=bC[:]))
            if stop_after in ("layer0", "peer_idx", "peer_g2"):
                stopped = True

        block = top.enter_context(nc.Block())
        s.emit(sems, block)
    return nc, len(s.ops)


def _bucket(d):
    d = np.maximum(d, 0)
    large = 16 + (np.log(np.maximum(d, 1).astype(np.float32) / np.float32(16))
                  / np.float32(math.log(2048 / 16)) * np.float32(16)).astype(np.int32)
    large = np.minimum(large, 31)
    return np.where(d < 16, d, large).astype(np.int64)


def make_inputs(cfg, c, p):
    S, T, NCH, CR, PR, NK, ESH = cfg.S, cfg.T, cfg.NCH, cfg.CR, cfg.PR, cfg.NKEYS, cfg.ESH
    g, hq = c // 4, c % 4
    own = [2 * hq, 2 * hq + 1]
    order8 = own + [h for h in range(8) if h not in own]
    headsA = [g * 8 + h for h in order8]
    headsB = [g * 8 + h for h in own]
    xf = p["x"].reshape(T, D)
    m = {}
    m["x"] = np.concatenate([xf[k * CR + c * PR:k * CR + (c + 1) * PR] for k in range(NCH)], 0)
    m["an"] = p["attn_norm"]; m["fn"] = p["ffn_norm"]; m["fin"] = p["final_norm"].reshape(1, D)
    cols = []
    for h in headsA:
        cols += list(range(h * 64, (h + 1) * 64))
    for h in headsB:
        cols += list(range(1840 + h * 64, 1840 + (h + 1) * 64))
    for base in (1024, 1152, 1280, 1536, 2864):
        cols += list(range(base + g * 64, base + (g + 1) * 64))
    for base in (1408, 1664, 2992):
        cols += list(range(base + g * 64, base + (g + 1) * 64))
    for h in own:
        cols += list(range(1792 + (g * 8 + h) * 3, 1792 + (g * 8 + h) * 3 + 3))
    cols = np.asarray(cols)
    m["wsel"] = p["w_in"][:, :, cols]
    rws = list(range((g * 8 + own[0]) * 64, (g * 8 + own[0]) * 64 + 128))
    rws += [1024 + r for r in rws]
    m["wo"] = p["w_out"][:, rws, :]
    m["wq_full"] = p["peer_wq"];
    for l_ in range(DEPTH):
        m[f"pu_full{l_}"] = p["peer_u"][l_]; m[f"pv_full{l_}"] = p["peer_v"][l_]
    m["skT"] = np.transpose(p["peer_subkeys"], (0, 1, 3, 2))
    m["w1k"] = p["cmp_w1_k"]; m["w1v"] = p["cmp_w1_v"]; m["w2k"] = p["cmp_w2_k"]; m["w2v"] = p["cmp_w2_v"]
    m["posk"] = p["cmp_pos_k"].reshape(DEPTH, 2048)
    m["posv"] = p["cmp_pos_v"].reshape(DEPTH, 2048)
    m["sink2"] = p["sinks"][:, headsB]
    rb = p["rel_bias"]
    m["b31c"] = rb[31:32, headsA]; m["b31s"] = rb[31:32, [16 + h for h in headsB]]
    ql = np.arange(128)[:, None]; xx = np.arange(NTAB)[None, :]
    dist_c = ql + 16 * (NTAB - 1 - xx) - 127
    m["tcr"] = np.stack([rb[_bucket(dist_c), h] for h in headsA], 0)
    m["mc"] = np.where(dist_c >= 0, 0.0, -30000.0)
    kk = np.arange(128)[:, None]; qq = np.arange(128)[None, :]
    m["tselr"] = np.stack([np.stack([rb[_bucket(128 * r + qq - kk), h] for h in headsA[:2]], 1) for r in range(NREL)], 0)
    m["tswar"] = np.stack([np.stack([rb[_bucket(128 * r + qq - kk), 16 + h] for h in headsB], 1) for r in range(2)], 0)
    m["m0"] = np.where(qq >= kk, 0.0, BIGNEG)
    m["m1"] = np.where(qq < kk, 0.0, BIGNEG)
    m["gexp"] = (np.arange(8192)[None, :] // 64 == np.arange(128)[:, None])
    m["ident"] = np.eye(128)
    m["iota16"] = np.tile(np.arange(16)[None, :], (128, 1))
    bon = np.zeros((128, 3), np.float32)
    bon[:64, 0] = 1e4; bon[:, 1] = 1e4; bon[:64, 2] = -1e30; bon[64:, 2] = 1e4
    m["bon"] = bon
    return {k: np.ascontiguousarray(np.asarray(v, dtype=np.float32)) for k, v in m.items()}


def run(cfg, p, stop_after=None):
    nc, nops = build_nc(cfg, stop_after)
    in_maps = [make_inputs(cfg, c, p) for c in range(N_CORES)]
    res = run_bass_kernel_spmd(nc, in_maps, core_ids=list(range(N_CORES)))
    T, NCH, CR, PR = cfg.T, cfg.NCH, cfg.CR, cfg.PR
    out = np.zeros((T, D), np.float32)
    for c in range(N_CORES):
        yl = res.results[c]["y"]
        for k in range(NCH):
            out[k * CR + c * PR:k * CR + (c + 1) * PR] = yl[k * PR:(k + 1) * PR]
    return out.reshape(cfg.B, cfg.S, D), res


def kernel(x, attn_norm, w_in, cmp_pos_k, cmp_w1_k, cmp_w2_k, cmp_pos_v, cmp_w1_v, cmp_w2_v,
           sinks, w_out, ffn_norm, peer_wq, peer_subkeys, peer_u, peer_v, rel_bias, final_norm):
    p = dict(x=x, attn_norm=attn_norm, w_in=w_in, cmp_pos_k=cmp_pos_k, cmp_w1_k=cmp_w1_k, cmp_w2_k=cmp_w2_k,
             cmp_pos_v=cmp_pos_v, cmp_w1_v=cmp_w1_v, cmp_w2_v=cmp_w2_v, sinks=sinks, w_out=w_out,
             ffn_norm=ffn_norm, peer_wq=peer_wq, peer_subkeys=peer_subkeys, peer_u=peer_u, peer_v=peer_v,
             rel_bias=rel_bias, final_norm=final_norm)
    p = {k: np.asarray(v, dtype=np.float32) for k, v in p.items()}
    B, S, _ = p["x"].shape
    cfg = Cfg(S=S, NKEYS=p["peer_subkeys"].shape[2], B=B)
    out, _ = run(cfg, p)
    return out.astype(np.float32)
```
